# Optimizing a Trainium2 kernel written in Bass

```python
import jax, jax.numpy as jnp
from jax import lax
import numpy as np

D_MODEL = 2048
BATCH = 2
SEQ = 4096
DEPTH = 2

N_A_LAYERS = DEPTH // 2
N_B_LAYERS = DEPTH - N_A_LAYERS
EPS = 1e-6
GLA_HEADS = 4
GLA_DK = D_MODEL // 2 // GLA_HEADS
GLA_DV = D_MODEL // GLA_HEADS
GLA_GATE_RANK = 16
GLA_GATE_TAU = 16.0
GLA_CHUNK = 64
GLA_QK_W = GLA_HEADS * GLA_DK
GLA_V_W = GLA_HEADS * GLA_DV
GLA_IN_W = 2 * GLA_QK_W + 2 * GLA_V_W + GLA_GATE_RANK
SB_HEADS = 16
SB_HEAD_DIM = D_MODEL // SB_HEADS
SB_W = SB_HEADS * SB_HEAD_DIM
SB_BLOCK = 128
D_FF = -(-8 * D_MODEL // (3 * 256)) * 256

kernel_name = "yoco_gla_stick_breaking_hybrid"


def rmsnorm(x, w):
    xf = x.astype(jnp.float32)
    xf = xf * lax.rsqrt(jnp.mean(xf * xf, axis=-1, keepdims=True) + EPS)
    return xf.astype(x.dtype) * w


def split_heads(t, n_heads):
    b, s, _ = t.shape
    return t.reshape(b, s, n_heads, -1).transpose(0, 2, 1, 3)


def swiglu(h, w_gate_up, w_down):
    gate, up = jnp.split(h @ w_gate_up, 2, axis=-1)
    return (jax.nn.silu(gate) * up) @ w_down


def gla_chunked(q, k, v, g):
    out_dtype = v.dtype
    q, k, v, g = (t.astype(jnp.float32) for t in (q, k, v, g))
    B, H, S, DK = q.shape
    DV = v.shape[-1]
    C = GLA_CHUNK
    NC = S // C

    def to_chunks(t):
        return t.reshape(B, H, NC, C, t.shape[-1]).transpose(2, 0, 1, 3, 4)

    causal = jnp.tril(jnp.ones((C, C), dtype=bool))[:, :, None]

    def step(state, inp):
        qi, ki, vi, gi = inp
        b = jnp.cumsum(gi, axis=-2)
        o_inter = jnp.einsum('bhck,bhkv->bhcv', qi * jnp.exp(b), state)
        rel = b[..., :, None, :] - b[..., None, :, :]
        decay = jnp.where(causal, jnp.exp(jnp.minimum(rel, 0.0)), 0.0)
        scores = jnp.einsum('bhik,bhjk,bhijk->bhij', qi, ki, decay)
        o_intra = jnp.einsum('bhij,bhjv->bhiv', scores, vi)
        b_last = b[..., -1:, :]
        k_dec = ki * jnp.exp(b_last - b)
        new_state = state * jnp.exp(b_last)[..., 0, :, None] + jnp.einsum('bhck,bhcv->bhkv', k_dec, vi)
        return new_state, o_inter + o_intra

    state0 = jnp.zeros((B, H, DK, DV), jnp.float32)
    _, o = lax.scan(step, state0, (to_chunks(q), to_chunks(k), to_chunks(v), to_chunks(g)))
    return o.transpose(1, 2, 0, 3, 4).reshape(B, H, S, DV).astype(out_dtype)


def gla_mixer(h, w_in, w_gate_up, b_gate, gnorm_w, w_out):
    B, S, _ = h.shape
    proj = h @ w_in
    q, k, v, r, gl = jnp.split(
        proj, [GLA_QK_W, 2 * GLA_QK_W, 2 * GLA_QK_W + GLA_V_W, 2 * GLA_QK_W + 2 * GLA_V_W], axis=-1)
    log_alpha = jax.nn.log_sigmoid((gl @ w_gate_up + b_gate).astype(jnp.float32)) / GLA_GATE_TAU
    o = gla_chunked(split_heads(q, GLA_HEADS) * (GLA_DK ** -0.5), split_heads(k, GLA_HEADS),
                    split_heads(v, GLA_HEADS), split_heads(log_alpha, GLA_HEADS))
    o = rmsnorm(o.transpose(0, 2, 1, 3), gnorm_w)
    o = o * jax.nn.silu(r).reshape(B, S, GLA_HEADS, GLA_DV)
    return o.reshape(B, S, GLA_V_W) @ w_out


def stick_breaking_attention(q, k, v):
    B, H, S, hd = q.shape
    NB = S // SB_BLOCK
    qb = q.reshape(B, H, NB, SB_BLOCK, hd).transpose(2, 0, 1, 3, 4)
    kpos = jnp.arange(S)

    def block(args):
        qi, i = args
        z = jnp.einsum('bhqd,bhkd->bhqk', qi, k).astype(jnp.float32) * (hd ** -0.5)
        qpos = i * SB_BLOCK + jnp.arange(SB_BLOCK)
        mask = kpos[None, :] < qpos[:, None]
        log_fail = jnp.where(mask, jax.nn.log_sigmoid(-z), 0.0)
        after = lax.cumsum(log_fail, axis=log_fail.ndim - 1, reverse=True) - log_fail
        a = jnp.where(mask, jnp.exp(jax.nn.log_sigmoid(z) + after), 0.0)
        return jnp.einsum('bhqk,bhkd->bhqd', a.astype(v.dtype), v)

    o = lax.map(block, (qb, jnp.arange(NB)))
    return o.transpose(1, 2, 0, 3, 4).reshape(B, H, S, hd)


def setup_inputs(seed: int = 0) -> dict:
    key = jax.random.key(seed)
    ks = jax.random.split(key, 20)
    f32 = jnp.float32

    def nrm(k, shape, fan_in):
        return jax.random.normal(k, shape, f32) * (fan_in ** -0.5)

    def gain(k, shape):
        return 1.0 + 0.01 * jax.random.normal(k, shape, f32)

    return {
        "x": jax.random.normal(ks[0], (BATCH, SEQ, D_MODEL), f32),
        "attn_norm_w": gain(ks[1], (DEPTH, D_MODEL)),
        "ffn_norm_w": gain(ks[2], (DEPTH, D_MODEL)),
        "gla_w_in": nrm(ks[3], (N_A_LAYERS, D_MODEL, GLA_IN_W), D_MODEL),
        "gla_w_gate_up": nrm(ks[4], (N_A_LAYERS, GLA_GATE_RANK, GLA_QK_W), GLA_GATE_RANK),
        "gla_b_gate": 0.1 * jax.random.normal(ks[5], (N_A_LAYERS, GLA_QK_W), f32),
        "gla_gnorm_w": gain(ks[6], (N_A_LAYERS, GLA_DV)),
        "gla_w_out": nrm(ks[7], (N_A_LAYERS, GLA_V_W, D_MODEL), GLA_V_W),
        "kv_norm_w": gain(ks[8], (D_MODEL,)),
        "sb_w_kv": nrm(ks[9], (D_MODEL, 2 * SB_W), D_MODEL),
        "sb_w_q": nrm(ks[10], (N_B_LAYERS, D_MODEL, SB_W), D_MODEL),
        "sb_w_out": nrm(ks[11], (N_B_LAYERS, SB_W, D_MODEL), SB_W),
        "ffn_w_gate_up": nrm(ks[12], (DEPTH, D_MODEL, 2 * D_FF), D_MODEL),
        "ffn_w_down": nrm(ks[13], (DEPTH, D_FF, D_MODEL), D_FF),
        "final_norm_w": gain(ks[14], (D_MODEL,)),
    }


def reference(x, attn_norm_w, ffn_norm_w, gla_w_in, gla_w_gate_up, gla_b_gate, gla_gnorm_w, gla_w_out,
              kv_norm_w, sb_w_kv, sb_w_q, sb_w_out, ffn_w_gate_up, ffn_w_down, final_norm_w):
    h = x
    k_shared = None
    v_shared = None
    for layer in range(DEPTH):
        a = rmsnorm(h, attn_norm_w[layer])
        if layer < N_A_LAYERS:
            i = layer
            h = h + gla_mixer(a, gla_w_in[i], gla_w_gate_up[i], gla_b_gate[i], gla_gnorm_w[i], gla_w_out[i])
        else:
            j = layer - N_A_LAYERS
            q = split_heads(a @ sb_w_q[j], SB_HEADS)
            o = stick_breaking_attention(q, k_shared, v_shared)
            o = o.transpose(0, 2, 1, 3).reshape(h.shape[0], h.shape[1], SB_W)
            h = h + o @ sb_w_out[j]
        h = h + swiglu(rmsnorm(h, ffn_norm_w[layer]), ffn_w_gate_up[layer], ffn_w_down[layer])
        if layer == N_A_LAYERS - 1:
            kv = rmsnorm(h, kv_norm_w) @ sb_w_kv
            k_c, v_c = jnp.split(kv, 2, axis=-1)
            k_shared = split_heads(k_c, SB_HEADS)
            v_shared = split_heads(v_c, SB_HEADS)
    return rmsnorm(h, final_norm_w)
```

```python
import numpy as np
from contextlib import ExitStack
import concourse.bass as bass
import concourse.mybir as mybir
from concourse.bass_utils import run_bass_kernel_spmd

F32 = mybir.dt.float32
BF16 = mybir.dt.bfloat16
AF = mybir.ActivationFunctionType
ALU = mybir.AluOpType

D = 2048
KC = 16
DFF = 5632
JC = 44
EPS = 1e-6
ENGS = ["pe", "act", "dve", "pool", "sp"]
SAME_ENG_SYNC = True


class Op:
    __slots__ = ("eng", "fn", "deps", "needs_inc", "dma_key", "val", "sem", "inc")

    def __init__(self, eng, fn, dma_key):
        self.eng = eng
        self.fn = fn
        self.deps = []
        self.needs_inc = False
        self.dma_key = dma_key
        self.val = 0
        self.sem = None


class Prog:
    def __init__(self, nc, stack):
        self.nc = nc
        self.stack = stack
        self.ops = {e: [] for e in ENGS}
        self.last_w = {}
        self.readers = {}
        self.dma_cnt = {}
        self.dma_sem = {}
        self.eng_sem = {}
        for e in ENGS:
            self.eng_sem[e] = stack.enter_context(nc.semaphore("prog_" + e))
        self.nps = 0

    def sb(self, name, shape, dtype):
        return self.stack.enter_context(self.nc.sbuf_tensor(name, list(shape), dtype))

    def ps(self, name, shape, dtype=F32):
        return self.stack.enter_context(self.nc.psum_tensor(name, list(shape), dtype))

    def add(self, eng, fn, reads=(), writes=(), dma_key=None, after=(), inc=16):
        op = Op(eng, fn, dma_key)
        op.inc = inc
        deps = {}

        def dep(d, kind):
            if d is None or d is op:
                return
            if d.dma_key is None and d.eng == eng:
                if eng == "pe" or kind == "war" or not SAME_ENG_SYNC:
                    return
            deps[id(d)] = d

        for r in reads:
            dep(self.last_w.get(r), "raw")
        for r in writes:
            dep(self.last_w.get(r), "waw")
            for rd in self.readers.get(r, ()):
                dep(rd, "war")
        for r in after:
            dep(self.last_w.get(r), "waw")
            for rd in self.readers.get(r, ()):
                dep(rd, "waw")
        for r in reads:
            self.readers.setdefault(r, []).append(op)
        for r in writes:
            self.last_w[r] = op
            self.readers[r] = []
        op.deps = list(deps.values())
        for d in op.deps:
            if d.dma_key is None:
                d.needs_inc = True
        if dma_key is not None:
            if dma_key not in self.dma_sem:
                self.dma_sem[dma_key] = self.stack.enter_context(
                    self.nc.semaphore("dma_%d" % len(self.dma_sem)))
                self.dma_cnt[dma_key] = 0
            self.dma_cnt[dma_key] += 1
            op.sem = self.dma_sem[dma_key]
            op.val = inc * self.dma_cnt[dma_key]
        else:
            op.sem = self.eng_sem[eng]
        self.ops[eng].append(op)
        return op

    def matmul(self, out, lhsT, rhs, start, stop, reads, writes):
        return self.add("pe", lambda e: e.matmul(out, lhsT, rhs, start=start, stop=stop),
                        reads, writes)

    def dma(self, eng, out, in_, reads, writes, key, after=()):
        return self.add(eng, lambda e: e.dma_start(out=out, in_=in_), reads, writes,
                        dma_key=key, after=after)

    def activation(self, eng, out, in_, func, reads, writes, bias=None, scale=None,
                   accum_out=None, after=()):
        kw = {}
        if bias is not None:
            kw["bias"] = bias
        if scale is not None:
            kw["scale"] = scale
        if accum_out is not None:
            kw["accum_out"] = accum_out
        return self.add(eng, lambda e: e.activation(out=out, in_=in_, func=func, **kw),
                        reads, writes, after=after)

    def tt(self, eng, out, in0, in1, op, reads, writes, after=()):
        return self.add(eng, lambda e: e.tensor_tensor(out=out, in0=in0, in1=in1, op=op),
                        reads, writes, after=after)

    def ts(self, eng, out, in0, s1, s2, op0, op1, reads, writes, after=()):
        if s2 is None:
            return self.add(eng, lambda e: e.tensor_scalar(out=out, in0=in0, scalar1=s1,
                                                           scalar2=None, op0=op0),
                            reads, writes, after=after)
        return self.add(eng, lambda e: e.tensor_scalar(out=out, in0=in0, scalar1=s1, scalar2=s2,
                                                       op0=op0, op1=op1), reads, writes,
                        after=after)

    def stt(self, eng, out, in0, scalar, in1, op0, op1, reads, writes, after=()):
        return self.add(eng, lambda e: e.scalar_tensor_tensor(out=out, in0=in0, scalar=scalar,
                                                              in1=in1, op0=op0, op1=op1),
                        reads, writes, after=after)

    def copy(self, eng, out, in_, reads, writes, after=()):
        if eng == "act":
            return self.add(eng, lambda e: e.activation(out=out, in_=in_, func=AF.Copy), reads,
                            writes, after=after)
        return self.add(eng, lambda e: e.tensor_copy(out=out, in_=in_), reads, writes,
                        after=after)

    def memset(self, eng, ap, val, writes):
        return self.add(eng, lambda e: e.memset(ap, val), (), writes)

    def allgather(self, out_t, in_t, reads, writes, groups=None):
        groups = groups or [list(range(8))]
        self.ncc = getattr(self, "ncc", 0) + 1
        return self.add("pool", lambda e: e.collective_compute(
            "AllGather", ALU.bypass, replica_groups=groups, ins=[in_t.ap().opt()],
            outs=[out_t.ap().opt()]), reads, writes, dma_key=("cc", self.ncc), inc=1)

    def wait_all(self, eng, keys):
        return self.add(eng, None, (), (), after=keys)

    def emit(self):
        nc = self.nc
        for e in ENGS:
            cnt = 0
            for op in self.ops[e]:
                if op.dma_key is None:
                    if op.needs_inc:
                        assert op.fn is not None
                        cnt += 1
                    op.val = cnt

        def body(e):
            def f(eng):
                waited = {}
                for op in self.ops[e]:
                    need = {}
                    for d in op.deps:
                        k = id(d.sem)
                        if k not in need or need[k][1] < d.val:
                            need[k] = (d.sem, d.val)
                    for k, (s, v) in need.items():
                        if waited.get(k, 0) < v:
                            eng.wait_ge(s, v)
                            waited[k] = v
                    if op.fn is not None:
                        inst = op.fn(eng)
                        if op.dma_key is not None and op.inc == 1:
                            inst.then_inc(op.sem)
                        elif op.dma_key is not None:
                            inst.then_inc(op.sem, 16)
                        elif op.needs_inc:
                            inst.then_inc(op.sem, 1)
            return f

        with nc.Block() as block:
            block.tensor(body("pe"))
            block.scalar(body("act"))
            block.vector(body("dve"))
            block.gpsimd(body("pool"))
            block.sync(body("sp"))


class Rot:
    def __init__(self, items):
        self.items = list(items)
        self.i = 0

    def next(self):
        x = self.items[self.i % len(self.items)]
        self.i += 1
        return x


def rmsnorm_stats_to_rstd(P, ss_banks, rstd, T, dim=D, keys=None):
    for hf in range(T // 512):
        sl = slice(hf * 512, (hf + 1) * 512)
        P.activation("act", rstd[:, sl], ss_banks[hf][:, :], AF.Sqrt, scale=1.0 / dim, bias=EPS,
                     reads=[keys[hf] if keys else ("bank", id(ss_banks[hf]))], writes=[("rstd", hf)])
        o_, i_ = rstd[:, sl], rstd[:, sl]
        P.add("dve", lambda e, o_=o_, i_=i_: e.reciprocal(out=o_, in_=i_),
              reads=[("rstd", hf)], writes=[("rstd", hf)])


def build_ffn_phase(final, extra_norms, T=1024, ag=False):
    nc = bass.Bass("TRN2", target_bir_lowering=False)
    NH = T // 512
    oT_d = nc.dram_tensor("oT", [D, T], BF16, kind="ExternalInput").ap()
    hT_d = nc.dram_tensor("hT", [D, T], F32, kind="ExternalInput").ap()
    wsh, wbn, wfl = {}, {}, {}
    for nm, (r, c) in (("w_out", (D, D)), ("w_gu", (D, 2 * DFF)), ("w_dn", (DFF, D))):
        if ag:
            wsh[nm] = nc.dram_tensor(nm, [r // 8, c], F32, kind="ExternalInput")
            wbn[nm] = nc.dram_tensor(nm + "_bn", [r // 8, c], F32)
            wfl[nm] = nc.dram_tensor(nm + "_full", [r, c], F32)
        else:
            wfl[nm] = nc.dram_tensor(nm, [r, c], F32, kind="ExternalInput")
    w_out_d, w_gu_d, w_dn_d = wfl["w_out"].ap(), wfl["w_gu"].ap(), wfl["w_dn"].ap()
    n_post = 2 if extra_norms else (1 if final else 0)
    nw_d = nc.dram_tensor("nw", [128, (1 + n_post) * KC], F32, kind="ExternalInput").ap()
    h1_d = nc.dram_tensor("h1_scr", [D, T], F32, kind="Internal").ap()
    h2_d = nc.dram_tensor("h2T", [D, T], F32, kind="ExternalOutput").ap()
    post_d = []
    for i in range(n_post):
        post_d.append(nc.dram_tensor("post%d" % i, [D, T], BF16 if extra_norms else F32,
                                     kind="ExternalOutput").ap())

    with ExitStack() as stack:
        P = Prog(nc, stack)
        big = P.sb("big", [128, JC * T], BF16)
        act = big[:, :].rearrange("p (j t) -> p j t", j=JC)
        hres = big[:, 0:KC * T * 2].bitcast(F32).rearrange("p (c t) -> p c t", c=KC)
        hn = P.sb("hn", [128, KC, T], BF16)
        NW = 3
        wslot = [P.sb("wslot%d" % i, [128, 11264], BF16) for i in range(NW)]
        wrot = Rot(range(NW))
        rstd = P.sb("rstd", [128, T], F32)
        sq = [P.sb("sq%d" % i, [128, 512], BF16) for i in range(2)]
        sqrot = Rot(range(2))
        sil = [P.sb("sil%d" % i, [128, 512], F32) for i in range(2)]
        silrot = Rot(range(2))
        h1t = [P.sb("h1t%d" % i, [128, 512], F32) for i in range(2)]
        h1rot = Rot(range(2))
        ot = [P.sb("ot%d" % i, [128, 512], F32) for i in range(2)]
        otrot = Rot(range(2))
        nw = P.sb("nwsb", [128, 1 + n_post, KC], F32)
        ones = P.sb("ones", [128, 128], BF16)
        banks = [P.ps("bank%d" % i, [128, 512], F32) for i in range(8)]

        def bk(i):
            return ("bank", id(banks[i]))

        hT_v = hT_d.rearrange("(c p) t -> p c t", p=128)
        oT_v = oT_d.rearrange("(c p) t -> p c t", p=128)
        h1_v = h1_d.rearrange("(c p) t -> p c t", p=128)
        h2_v = h2_d.rearrange("(c p) t -> p c t", p=128)

        for nm in ("w_out", "w_gu", "w_dn"):
            if ag:
                P.dma("sp", wbn[nm].ap(), wsh[nm].ap(), [], [("wbn", nm)], key=("wbn", nm))
        for nm in ("w_out", "w_gu", "w_dn"):
            if ag:
                P.allgather(wfl[nm], wbn[nm], [("wbn", nm)], [("wfull", nm)])
        P.memset("pool", ones[:, :], 1.0, writes=["ones"])
        P.dma("sp", nw[:, :, :], nw_d.rearrange("p (n c) -> p n c", c=KC), [], ["nw"], key="nw")
        for c in range(KC):
            P.dma("sp", hres[:, c, :], hT_v[:, c, :], [], [("h", c)], key=("h", c))
            P.dma("pool", hn[:, c, :], oT_v[:, c, :], [], [("hn", c)], key=("hn", c))

        w_out_v = w_out_d.rearrange("(kc p) n -> p kc n", p=128)
        ssb = [6, 7]
        brot = Rot([(0, 1), (2, 3), (4, 5)])
        for dg in range(8):
            s = wrot.next()
            wt = wslot[s][:, 0:KC * 256].rearrange("p (k n) -> p k n", k=KC)
            P.dma("pool", wt, w_out_v[:, :, dg * 256:(dg + 1) * 256], [("wfull", "w_out")], [("w", s)],
                  key=("w", s))
            for dd in range(2):
                d = dg * 2 + dd
                bset = brot.next()
                for k in range(KC):
                    for hf in range(NH):
                        P.matmul(banks[bset[hf]][:, :], wt[:, k, dd * 128:(dd + 1) * 128],
                                 hn[:, k, hf * 512:(hf + 1) * 512], k == 0, k == KC - 1,
                                 reads=[("w", s), ("hn", k)], writes=[bk(bset[hf])])
                for hf in range(NH):
                    sl = slice(hf * 512, (hf + 1) * 512)
                    P.tt("dve", hres[:, d, sl], banks[bset[hf]][:, :], hres[:, d, sl], ALU.add,
                         reads=[bk(bset[hf]), ("h", d)], writes=[("h", d)])
                    q = sqrot.next()
                    P.activation("act", sq[q][:, :], hres[:, d, sl], AF.Square,
                                 reads=[("h", d)], writes=[("sq", q)])
                    P.matmul(banks[ssb[hf]][:, :], ones[:, :], sq[q][:, :], d == 0, d == KC - 1,
                             reads=["ones", ("sq", q)], writes=[bk(ssb[hf])])
                P.dma("sp", h1_v[:, d, :], hres[:, d, :], [("h", d)], [("h1d", d)], key=("h1d", d))

        rmsnorm_stats_to_rstd(P, [banks[6], banks[7]], rstd, T)
        erot = Rot(["dve"])
        for c in range(KC):
            P.stt(erot.next(), hn[:, c, :], hres[:, c, :], nw[:, 0, c:c + 1], rstd[:, :],
                  ALU.mult, ALU.mult,
                  reads=[("h", c), "nw", ("rstd", 0), ("rstd", 1)], writes=[("hn", c)])

        w_gu_v = w_gu_d.rearrange("(kc p) n -> p kc n", p=128)
        b4rot = Rot([(0, 1, 2, 3), (4, 5, 6, 7)])
        for jp in range(JC // 2):
            s = wrot.next()
            wg = wslot[s][:, 0:KC * 256].rearrange("p (k n) -> p k n", k=KC)
            wu = wslot[s][:, KC * 256:2 * KC * 256].rearrange("p (k n) -> p k n", k=KC)
            P.dma("pool", wg, w_gu_v[:, :, jp * 256:(jp + 1) * 256], [("wfull", "w_gu")], [("w", s)],
                  key=("w", s))
            P.dma("pool", wu, w_gu_v[:, :, DFF + jp * 256:DFF + (jp + 1) * 256], [("wfull", "w_gu")],
                  [("w", s, 1)], key=("w", s, 1))
            for jj in range(2):
                j = jp * 2 + jj
                bset = b4rot.next()
                for gi, wv in enumerate((wg, wu)):
                    for k in range(KC):
                        for hf in range(NH):
                            b = bset[gi * 2 + hf]
                            P.matmul(banks[b][:, :], wv[:, k, jj * 128:(jj + 1) * 128],
                                     hn[:, k, hf * 512:(hf + 1) * 512], k == 0, k == KC - 1,
                                     reads=[("w", s), ("w", s, 1), ("hn", k)], writes=[bk(b)])
                for hf in range(NH):
                    sl = slice(hf * 512, (hf + 1) * 512)
                    si = silrot.next()
                    P.activation("act", sil[si][:, :], banks[bset[hf]][:, :], AF.Silu,
                                 reads=[bk(bset[hf])], writes=[("sil", si)])
                    P.tt("dve", act[:, j, sl], banks[bset[2 + hf]][:, :], sil[si][:, :], ALU.mult,
                         reads=[bk(bset[2 + hf]), ("sil", si)], writes=[("act", j)],
                         after=[("h", j // 2)])

        w_dn_v = w_dn_d.rearrange("(jc p) n -> p jc n", p=128)
        ssb = [6, 7]
        brot = Rot([(0, 1), (2, 3), (4, 5)])
        for dg in range(8):
            s = wrot.next()
            wt = wslot[s][:, 0:JC * 256].rearrange("p (k n) -> p k n", k=JC)
            P.dma("pool", wt[:, 0:22, :], w_dn_v[:, 0:22, dg * 256:(dg + 1) * 256], [("wfull", "w_dn")],
                  [("w", s)], key=("w", s))
            P.dma("pool", wt[:, 22:44, :], w_dn_v[:, 22:44, dg * 256:(dg + 1) * 256], [("wfull", "w_dn")],
                  [("w", s, 1)], key=("w", s, 1))
            for dd in range(2):
                d = dg * 2 + dd
                bset = brot.next()
                for j in range(JC):
                    for hf in range(NH):
                        P.matmul(banks[bset[hf]][:, :], wt[:, j, dd * 128:(dd + 1) * 128],
                                 act[:, j, hf * 512:(hf + 1) * 512], j == 0, j == JC - 1,
                                 reads=[("w", s), ("w", s, 1), ("act", j)], writes=[bk(bset[hf])])
                for hf in range(NH):
                    sl = slice(hf * 512, (hf + 1) * 512)
                    hi = h1rot.next()
                    P.dma("sp", h1t[hi][:, :], h1_v[:, d, sl], [("h1d", d)], [("h1t", hi)],
                          key=("h1t", hi))
                    oi = otrot.next()
                    P.tt("dve", ot[oi][:, :], banks[bset[hf]][:, :], h1t[hi][:, :], ALU.add,
                         reads=[bk(bset[hf]), ("h1t", hi)], writes=[("ot", oi)])
                    P.dma("sp", h2_v[:, d, sl], ot[oi][:, :], [("ot", oi)], [("h2d", d, hf)],
                          key=("ot", oi))
                    if n_post:
                        q = sqrot.next()
                        P.activation("act", sq[q][:, :], ot[oi][:, :], AF.Square,
                                     reads=[("ot", oi)], writes=[("sq", q)])
                        P.matmul(banks[ssb[hf]][:, :], ones[:, :], sq[q][:, :], d == 0, d == KC - 1,
                                 reads=["ones", ("sq", q)], writes=[bk(ssb[hf])])

        if n_post:
            rmsnorm_stats_to_rstd(P, [banks[6], banks[7]], rstd, T)
            pbuf = [big[:, i * 4 * T:(i * 4 + 2) * T].bitcast(F32) for i in range(4)]
            if extra_norms:
                obuf = [big[:, (i * 4 + 2) * T:(i * 4 + 3) * T] for i in range(4)]
            else:
                obuf = [big[:, (i * 4 + 2) * T:(i * 4 + 4) * T].bitcast(F32) for i in range(4)]
            prot = Rot(range(4))
            orot = Rot(range(4))
            allact = [("act", j) for j in range(JC)]
            for d in range(KC):
                pi = prot.next()
                P.dma("sp", pbuf[pi], h2_v[:, d, :], [("h2d", d, 0), ("h2d", d, 1)], [("pbuf", pi)],
                      key=("pbuf", pi), after=allact)
                for n in range(n_post):
                    oi = orot.next()
                    P.stt(erot.next(), obuf[oi], pbuf[pi], nw[:, 1 + n, d:d + 1], rstd[:, :],
                          ALU.mult, ALU.mult,
                          reads=[("pbuf", pi), "nw", ("rstd", 0), ("rstd", 1)], writes=[("obuf", oi)],
                          after=allact)
                    P.dma("sp", post_d[n].rearrange("(c p) t -> p c t", p=128)[:, d, :], obuf[oi],
                          [("obuf", oi)], [("postd", n, d)], key=("obuf", oi))
        outs = [("h2d", d, hf) for d in range(KC) for hf in range(NH)]
        outs += [("postd", n, d) for n in range(n_post) for d in range(KC)]
        P.wait_all("sp", outs)
        P.emit()
    return nc


def pack_norm_w(ws):
    arr = np.stack([np.asarray(w, np.float32).reshape(KC, 128).T for w in ws], axis=1)
    return np.ascontiguousarray(arr.reshape(128, -1))


GDK, GDV, GC = 256, 512, 64
WHC = 2 * GDK + 2 * GDV + 16


def gla_consts():
    j = np.arange(128)[:, None]
    i = np.arange(128)[None, :]
    same = (j // 64) == (i // 64)
    msuf = (same & (j > i)).astype(np.float32)
    cm = ((j % 64) <= np.arange(64)[None, :]).astype(np.float32)
    rm = np.ones((128, 512), np.float32)
    rm[:, ::64] = 0.0
    return {"msuf": msuf, "cmask": np.ascontiguousarray(cm), "rmask": rm}


def build_gla_phase(S=4096):
    nc = bass.Bass("TRN2", target_bir_lowering=False)
    NB = S // 512
    xT_d = nc.dram_tensor("xT", [D, S], F32, kind="ExternalInput").ap()
    wh_d = nc.dram_tensor("wh", [D, WHC], F32, kind="ExternalInput").ap()
    nw_d = nc.dram_tensor("nw", [128, KC], F32, kind="ExternalInput").ap()
    gw_d = nc.dram_tensor("gw", [128, 4], F32, kind="ExternalInput").ap()
    wgu_d = nc.dram_tensor("wgu", [17, GDK], F32, kind="ExternalInput").ap()
    msuf_d = nc.dram_tensor("msuf", [128, 128], F32, kind="ExternalInput").ap()
    cmask_d = nc.dram_tensor("cmask", [128, 64], F32, kind="ExternalInput").ap()
    rmask_d = nc.dram_tensor("rmask", [128, 512], F32, kind="ExternalInput").ap()
    og_d = nc.dram_tensor("ogT", [GDV, S], BF16, kind="ExternalOutput").ap()

    with ExitStack() as stack:
        P = Prog(nc, stack)
        W = P.sb("W", [128, KC, WHC], BF16)
        xt = [P.sb("xt%d" % i, [128, KC, 512], F32) for i in range(1)]
        sqall = P.sb("sqall", [128, KC * 512], BF16)
        aT = P.sb("aT", [128, KC, 512], BF16)
        rstd = P.sb("rstd", [128, 512], F32)
        nw = P.sb("nwsb", [128, KC], F32)
        gw = P.sb("gwsb", [128, 4], F32)
        wgu = P.sb("wgusb", [32, GDK], F32)
        msuf = P.sb("msufsb", [128, 128], F32)
        cmask = P.sb("cmasksb", [128, 64], F32)
        rmask = P.sb("rmasksb", [128, 512], F32)
        ones = P.sb("ones", [128, 128], BF16)
        glaug = P.sb("glaug", [32, 512], F32)
        ex = [P.sb("ex%d" % i, [128, 512], F32) for i in range(2)]
        sp = [P.sb("sp%d" % i, [128, 512], F32) for i in range(2)]
        cs = [P.sb("cs%d" % i, [128, 512], F32) for i in range(2)]
        E1 = [P.sb("E1_%d" % i, [128, 512], F32) for i in range(2)]
        E2 = [P.sb("E2_%d" % i, [128, 512], F32) for i in range(2)]
        qe = [P.sb("qe%d" % i, [128, 512], BF16) for i in range(2)]
        ke = [P.sb("ke%d" % i, [128, 512], BF16) for i in range(2)]
        rsil = [P.sb("rsil%d" % i, [128, 512], F32) for i in range(4)]
        vtok = [P.sb("vtok%d" % i, [128, GDV], BF16) for i in range(4)]
        kd = [P.sb("kd%d" % i, [128, GDK], BF16) for i in range(4)]
        ext = [P.sb("ext%d" % i, [128, GDK], F32) for i in range(2)]
        spt = [P.sb("spt%d" % i, [128, GDK], F32) for i in range(4)]
        dk_ = [P.sb("dkt%d" % i, [128, GDK], F32) for i in range(2)]
        scsb = P.sb("scsb", [128, 64], BF16)
        S32 = [P.sb("S32_%d" % i, [128, GDV], F32) for i in range(2)]
        Sbf = [P.sb("Sbf_%d" % i, [128, GDV], BF16) for i in range(2)]
        ogf = [P.sb("ogf%d" % i, [128, 512], F32) for i in range(2)]
        ogb = [P.sb("ogb%d" % i, [128, 512], BF16) for i in range(2)]
        banks = [P.ps("bank%d" % i, [128, 512], F32) for i in range(8)]
        grot = Rot([0, 1, 2])
        SSB = 3

        def bk(i):
            return ("bank", i)

        P.memset("pool", ones[:, :], 1.0, writes=["ones"])
        P.memset("pool", glaug[:, :], 1.0, writes=["glaug"])
        for i in range(2):
            P.memset("pool", S32[i][:, :], 0.0, writes=[("S32", i)])
            P.memset("pool", Sbf[i][:, :], 0.0, writes=[("Sbf", i)])
        P.dma("sp", nw[:, :], nw_d, [], ["nw"], key="nw")
        P.dma("sp", gw[:, :], gw_d, [], ["gw"], key="gw")
        P.dma("sp", wgu[0:17, :], wgu_d, [], ["wgu"], key="wgu")
        P.dma("sp", msuf[:, :], msuf_d, [], ["msuf"], key="msuf")
        P.dma("sp", cmask[:, :], cmask_d, [], ["cmask"], key="cmask")
        P.dma("sp", rmask[:, :], rmask_d, [], ["rmask"], key="rmask")
        wh_v = wh_d.rearrange("(kc p) n -> p kc n", p=128)
        for k in range(KC):
            P.dma("pool", W[:, k, :], wh_v[:, k, :], [], [("W", k)], key=("W", k))
        xT_v = xT_d.rearrange("(c p) t -> p c t", p=128)
        og_v = og_d.rearrange("(c p) t -> p c t", p=128)
        Wk = [("W", k) for k in range(KC)]
        CQ, CK, CV, CR, CG = 0, GDK, 2 * GDK, 2 * GDK + GDV, 2 * GDK + 2 * GDV

        def load_x(n):
            xb = 0
            for c in range(KC):
                P.dma("sp", xt[xb][:, c, :], xT_v[:, c, n * 512:(n + 1) * 512], [],
                      [("xt", xb, c)], key=("xt", xb, c))

        def fm_proj(col0, M):
            b = grot.next()
            for k in range(KC):
                P.matmul(banks[b][0:M, :], W[:, k, col0:col0 + M], aT[:, k, :], k == 0, k == KC - 1,
                         reads=[("W", k), ("aT", k)], writes=[bk(b)])
            return b

        load_x(0)
        for n in range(NB):
            xb = 0
            xkeys = [("xt", xb, c) for c in range(KC)]
            P.activation("act", sqall[:, :], xt[xb][:, :, :].rearrange("p c t -> p (c t)"), AF.Square,
                         reads=xkeys, writes=["sqall"])
            for c in range(KC):
                P.matmul(banks[SSB][:, :], ones[:, :], sqall[:, c * 512:(c + 1) * 512], c == 0,
                         c == KC - 1, reads=["ones", "sqall"], writes=[bk(SSB)])
            rmsnorm_stats_to_rstd(P, [banks[SSB]], rstd, 512, keys=[bk(SSB)])
            for c in range(KC):
                P.stt("dve", aT[:, c, :], xt[xb][:, c, :], nw[:, c:c + 1], rstd[:, :], ALU.mult, ALU.mult,
                      reads=[("xt", xb, c), "nw", ("rstd", 0)], writes=[("aT", c)])
            if n + 1 < NB:
                load_x(n + 1)
            b = fm_proj(CG, 16)
            P.copy("dve", glaug[0:16, :], banks[b][0:16, :], reads=[bk(b)], writes=["glaug"])
            for dkc in range(2):
                b = grot.next()
                P.matmul(banks[b][:, :], wgu[0:17, dkc * 128:(dkc + 1) * 128], glaug[0:17, :], True, True,
                         reads=["wgu", "glaug"], writes=[bk(b)])
                P.activation("act", ex[dkc][:, :], banks[b][:, :], AF.Exp, scale=-1.0,
                             reads=[bk(b)], writes=[("ex", dkc)])
            for dkc in range(2):
                P.activation("act", sp[dkc][:, :], ex[dkc][:, :], AF.Ln, bias=1.0,
                             reads=[("ex", dkc)], writes=[("sp", dkc)])
                o_, m_, s_ = cs[dkc][:, :], rmask[:, :], sp[dkc][:, :]
                P.add("dve", lambda e, o_=o_, m_=m_, s_=s_: e.tensor_tensor_scan(
                    out=o_, data0=m_, data1=s_, initial=0.0, op0=ALU.mult, op1=ALU.add),
                    reads=["rmask", ("sp", dkc)], writes=[("cs", dkc)])
            for dkc in range(2):
                P.activation("act", E1[dkc][:, :], cs[dkc][:, :], AF.Exp, scale=-1.0 / 16,
                             reads=[("cs", dkc)], writes=[("E1", dkc)])
                P.activation("act", E2[dkc][:, :], cs[dkc][:, :], AF.Exp, scale=1.0 / 16,
                             reads=[("cs", dkc)], writes=[("E2", dkc)])
            for dkc in range(2):
                b = fm_proj(CQ + dkc * 128, 128)
                P.stt("dve", qe[dkc][:, :], banks[b][:, :], float(GDK) ** -0.5, E1[dkc][:, :],
                      ALU.mult, ALU.mult, reads=[bk(b), ("E1", dkc)], writes=[("qe", dkc)])
            for dkc in range(2):
                b = fm_proj(CK + dkc * 128, 128)
                P.tt("dve", ke[dkc][:, :], banks[b][:, :], E2[dkc][:, :], ALU.mult,
                     reads=[bk(b), ("E2", dkc)], writes=[("ke", dkc)])
            for tt in range(4):
                tsl = slice(tt * 128, (tt + 1) * 128)
                b = grot.next()
                for k in range(KC):
                    P.matmul(banks[b][:, :], aT[:, k, tsl], W[:, k, CV:CV + GDV], k == 0, k == KC - 1,
                             reads=[("W", k), ("aT", k)], writes=[bk(b)])
                P.copy("act", vtok[tt][:, :], banks[b][:, :], reads=[bk(b)], writes=[("vtok", tt)])
                bg = grot.next()
                P.matmul(banks[bg][:, 0:GDK], glaug[0:17, tsl], wgu[0:17, :], True, True,
                         reads=["wgu", "glaug"], writes=[bk(bg)])
                e_ = ext[tt % 2]
                P.activation("act", e_[:, :], banks[bg][:, 0:GDK], AF.Exp, scale=-1.0,
                             reads=[bk(bg)], writes=[("ext", tt % 2)])
                P.activation("act", spt[tt][:, :], e_[:, :], AF.Ln, bias=1.0,
                             reads=[("ext", tt % 2)], writes=[("spt", tt)])
            for tt in range(4):
                tsl = slice(tt * 128, (tt + 1) * 128)
                bs = grot.next()
                P.matmul(banks[bs][:, 0:GDK], msuf[:, :], spt[tt][:, :], True, True,
                         reads=["msuf", ("spt", tt)], writes=[bk(bs)])
                d_ = dk_[tt % 2]
                P.activation("act", d_[:, :], banks[bs][:, 0:GDK], AF.Exp, scale=-1.0 / 16,
                             reads=[bk(bs)], writes=[("dkt", tt % 2)])
                b = grot.next()
                for k in range(KC):
                    P.matmul(banks[b][:, 0:GDK], aT[:, k, tsl], W[:, k, CK:CK + GDK], k == 0, k == KC - 1,
                             reads=[("W", k), ("aT", k)], writes=[bk(b)])
                P.tt("dve", kd[tt][:, :], banks[b][:, 0:GDK], d_[:, :], ALU.mult,
                     reads=[bk(b), ("dkt", tt % 2)], writes=[("kd", tt)])
            for dvc in range(4):
                b = fm_proj(CR + dvc * 128, 128)
                P.activation("act", rsil[dvc][:, :], banks[b][:, :], AF.Silu,
                             reads=[bk(b)], writes=[("rsil", dvc)])
            for c in range(8):
                tt, r0 = c // 2, (c % 2) * 64
                rows = slice(r0, r0 + 64)
                cols = slice(c * 64, (c + 1) * 64)
                for dkc in range(2):
                    P.matmul(banks[SSB][rows, 0:64], ke[dkc][:, cols], qe[dkc][:, cols], dkc == 0, dkc == 1,
                             reads=[("ke", dkc), ("qe", dkc)], writes=[bk(SSB)])
                P.tt("dve", scsb[rows, :], banks[SSB][rows, 0:64], cmask[rows, :], ALU.mult,
                     reads=[bk(SSB), "cmask"], writes=["scsb"])
                for dvc in range(4):
                    dsl = slice(dvc * 128, (dvc + 1) * 128)
                    ob = banks[4 + dvc]
                    for dkc in range(2):
                        P.matmul(ob[:, cols], Sbf[dkc][:, dsl], qe[dkc][:, cols], dkc == 0, False,
                                 reads=[("Sbf", dkc), ("qe", dkc)], writes=[("ob", dvc)])
                    P.matmul(ob[:, cols], vtok[tt][rows, dsl], scsb[rows, :], False, True,
                             reads=[("vtok", tt), "scsb"], writes=[("ob", dvc)])
                for dkc in range(2):
                    b = grot.next()
                    P.matmul(banks[b][:, :], kd[tt][rows, dkc * 128:(dkc + 1) * 128], vtok[tt][rows, :],
                             True, True, reads=[("kd", tt), ("vtok", tt)], writes=[bk(b)])
                    lc = c * 64 + 63
                    P.stt("dve", S32[dkc][:, :], S32[dkc][:, :], E1[dkc][:, lc:lc + 1], banks[b][:, :],
                          ALU.mult, ALU.add, reads=[("S32", dkc), ("E1", dkc), bk(b)],
                          writes=[("S32", dkc)])
                    P.copy("pool", Sbf[dkc][:, :], S32[dkc][:, :], reads=[("S32", dkc)],
                           writes=[("Sbf", dkc)])
            for dvc in range(4):
                P.activation("act", sqall[:, dvc * 512:(dvc + 1) * 512], banks[4 + dvc][:, :], AF.Square,
                             reads=[("ob", dvc)], writes=["sqall"])
            for dvc in range(4):
                P.matmul(banks[SSB][:, :], ones[:, :], sqall[:, dvc * 512:(dvc + 1) * 512], dvc == 0,
                         dvc == 3, reads=["ones", "sqall"], writes=[bk(SSB)])
            rmsnorm_stats_to_rstd(P, [banks[SSB]], rstd, 512, dim=GDV, keys=[bk(SSB)])
            for dvc in range(4):
                P.stt("dve", ogf[dvc % 2][:, :], banks[4 + dvc][:, :], gw[:, dvc:dvc + 1], rstd[:, :],
                      ALU.mult, ALU.mult, reads=[("ob", dvc), "gw", ("rstd", 0)],
                      writes=[("ogf", dvc % 2)])
                P.tt("pool", ogb[dvc % 2][:, :], ogf[dvc % 2][:, :], rsil[dvc][:, :], ALU.mult,
                     reads=[("ogf", dvc % 2), ("rsil", dvc)], writes=[("ogb", dvc % 2)])
                P.dma("sp", og_v[:, dvc, n * 512:(n + 1) * 512], ogb[dvc % 2][:, :],
                      [("ogb", dvc % 2)], [("ogd", dvc, n)], key=("ogb", dvc % 2))
        P.wait_all("sp", [("ogd", dvc, n) for dvc in range(4) for n in range(NB)])
        P.emit()
    return nc


HD = 128
BIG = 30000.0


def sb_consts():
    p = np.arange(128)[:, None]
    f = np.arange(512)[None, :]
    m01 = np.stack([(p + 128 * off < f) for off in range(4)]).astype(np.float32)
    mbig = (1.0 - m01) * BIG
    tri = (np.arange(128)[:, None] >= np.arange(128)[None, :]).astype(np.float32)
    return {"m01": np.ascontiguousarray(m01.transpose(1, 0, 2).reshape(128, 4 * 512)),
            "mbig": np.ascontiguousarray(mbig.transpose(1, 0, 2).reshape(128, 4 * 512)),
            "tri": tri}


def build_sb_phase(S=4096):
    nc = bass.Bass("TRN2", target_bir_lowering=False)
    NB = S // 512
    NS = S // 128
    hk_d = nc.dram_tensor("hkT", [D, S], BF16, kind="ExternalInput").ap()
    hq_d = nc.dram_tensor("hqT", [D, S], BF16, kind="ExternalInput").ap()
    wk_d = nc.dram_tensor("wk", [D, 512], F32, kind="ExternalInput").ap()
    wv_d = nc.dram_tensor("wv", [D, 512], F32, kind="ExternalInput").ap()
    wq_d = nc.dram_tensor("wq", [D, 512], F32, kind="ExternalInput").ap()
    m01_d = nc.dram_tensor("m01", [128, 4 * 512], F32, kind="ExternalInput").ap()
    mbig_d = nc.dram_tensor("mbig", [128, 4 * 512], F32, kind="ExternalInput").ap()
    tri_d = nc.dram_tensor("tri", [128, 128], F32, kind="ExternalInput").ap()
    o_d = nc.dram_tensor("oT", [512, S], BF16, kind="ExternalOutput").ap()

    with ExitStack() as stack:
        P = Prog(nc, stack)
        nkT = P.sb("nkT", [128, 4, S], BF16)
        qT = P.sb("qT", [128, 4, S], BF16)
        vtok = P.sb("vtok", [128, NS, 512], BF16)
        ov = P.sb("ov", [128, 40960], BF16)
        wk = ov[:, 0:8192].rearrange("p (k n) -> p k n", k=KC)
        wv = ov[:, 8192:16384].rearrange("p (k n) -> p k n", k=KC)
        wq = ov[:, 16384:24576].rearrange("p (k n) -> p k n", k=KC)
        hs = [ov[:, 24576 + i * 8192:24576 + (i + 1) * 8192].rearrange("p (k n) -> p k n", k=KC)
              for i in range(2)]
        OVK = ["wk", "wv", "wq", ("hs", 0), ("hs", 1)]
        ef = [ov[:, i * 1024:(i + 1) * 1024].bitcast(F32) for i in range(2)]
        rbm = [ov[:, 2048 + i * 1024:2048 + (i + 1) * 1024].bitcast(F32) for i in range(2)]
        splf = [ov[:, 4096 + i * 1024:4096 + (i + 1) * 1024].bitcast(F32) for i in range(2)]
        splb = [ov[:, 6144 + i * 512:6144 + (i + 1) * 512] for i in range(3)]
        accb = [ov[:, 7680 + i * 512:7680 + (i + 1) * 512] for i in range(2)]
        ab = [ov[:, 8704 + i * 512:8704 + (i + 1) * 512] for i in range(3)]
        osb = [ov[:, 10240 + i * 512:10240 + (i + 1) * 512] for i in range(2)]
        m01 = P.sb("m01sb", [128, 4, 512], F32)
        mbig = P.sb("mbigsb", [128, 4, 512], F32)
        trif = P.sb("trif", [128, 128], F32)
        tri = P.sb("trib", [128, 128], BF16)
        ones = P.sb("ones", [128, 128], BF16)
        banks = [P.ps("bank%d" % i, [128, 512], F32) for i in range(8)]

        def bk(i):
            return ("bank", i)

        P.memset("pool", ones[:, :], 1.0, writes=["ones"])
        P.dma("sp", m01[:, :, :], m01_d.rearrange("p (o f) -> p o f", o=4), [], ["m01"], key="m01")
        P.dma("sp", mbig[:, :, :], mbig_d.rearrange("p (o f) -> p o f", o=4), [], ["mbig"], key="mbig")
        P.dma("sp", trif[:, :], tri_d, [], ["trif"], key="trif")
        P.copy("dve", tri[:, :], trif[:, :], reads=["trif"], writes=["tri"])
        for nm, wt, wd in (("wk", wk, wk_d), ("wv", wv, wv_d), ("wq", wq, wq_d)):
            P.dma("pool", wt, wd.rearrange("(kc p) n -> p kc n", p=128), [], [nm], key=nm)
        hk_v = hk_d.rearrange("(c p) t -> p c t", p=128)
        hq_v = hq_d.rearrange("(c p) t -> p c t", p=128)
        o_v = o_d.rearrange("(h p) t -> p h t", p=128)

        prot = Rot([0, 1, 2, 3])
        erot = Rot(["act", "dve"])
        for n in range(NB):
            nsl = slice(n * 512, (n + 1) * 512)
            P.dma("sp", hs[0], hk_v[:, :, nsl], [], [("hs", 0)], key=("hs", 0))
            P.dma("sp", hs[1], hq_v[:, :, nsl], [], [("hs", 1)], key=("hs", 1))
            for h in range(4):
                b = prot.next()
                for k in range(KC):
                    P.matmul(banks[b][:, :], wk[:, k, h * 128:(h + 1) * 128], hs[0][:, k, :], k == 0,
                             k == KC - 1, reads=["wk", ("hs", 0)], writes=[bk(b)])
                P.activation("act", nkT[:, h, nsl], banks[b][:, :], AF.Copy, scale=-1.0,
                             reads=[bk(b)], writes=[("nkT", h, n)])
            for st in range(4):
                b = prot.next()
                for k in range(KC):
                    P.matmul(banks[b][:, :], hs[0][:, k, st * 128:(st + 1) * 128], wv[:, k, :], k == 0,
                             k == KC - 1, reads=["wv", ("hs", 0)], writes=[bk(b)])
                P.copy("dve", vtok[:, n * 4 + st, :], banks[b][:, :], reads=[bk(b)],
                       writes=[("vtok", n * 4 + st)])
            for h in range(4):
                b = prot.next()
                for k in range(KC):
                    P.matmul(banks[b][:, :], wq[:, k, h * 128:(h + 1) * 128], hs[1][:, k, :], k == 0,
                             k == KC - 1, reads=["wq", ("hs", 1)], writes=[bk(b)])
                P.ts("dve", qT[:, h, nsl], banks[b][:, :], float(HD) ** -0.5, None, ALU.mult, None,
                     reads=[bk(b)], writes=[("qT", h, n)])

        zrot = Rot([0, 1])
        rrot = Rot([2, 3])
        orot = Rot([4, 5])
        efr, rbr, sfr, sbr, acr, abr, osr = (Rot(range(2)), Rot(range(2)), Rot(range(2)),
                                             Rot(range(3)), Rot(range(2)), Rot(range(3)),
                                             Rot(range(2)))
        for h in range(4):
            for T in range(NB):
                tsl = slice(T * 512, (T + 1) * 512)
                ob = orot.next()
                imax = 4 * T + 3
                acc_i = None
                for i in range(imax, -1, -1):
                    ssl = slice(i * 128, (i + 1) * 128)
                    off = i - 4 * T
                    kq_r = [("nkT", h, i // 4), ("qT", h, T)]
                    zb = zrot.next()
                    P.matmul(banks[zb][:, :], nkT[:, h, ssl], qT[:, h, tsl], True, True,
                             reads=kq_r, writes=[bk(zb)])
                    e_i = efr.next()
                    P.activation("act", ef[e_i], banks[zb][:, :], AF.Exp, scale=-1.0,
                                 reads=[bk(zb)], writes=[("ef", e_i)], after=OVK)
                    s_i = sbr.next()
                    if off >= 0:
                        f_i = sfr.next()
                        P.activation("act", splf[f_i], ef[e_i], AF.Ln, bias=1.0,
                                     reads=[("ef", e_i)], writes=[("splf", f_i)], after=OVK)
                        P.tt("pool", splb[s_i], splf[f_i], m01[:, off, :], ALU.mult,
                             reads=[("splf", f_i), "m01"], writes=[("splb", s_i)], after=OVK)
                    else:
                        P.activation("act", splb[s_i], ef[e_i], AF.Ln, bias=1.0,
                                     reads=[("ef", e_i)], writes=[("splb", s_i)], after=OVK)
                    rb = rrot.next()
                    P.matmul(banks[rb][:, :], tri[:, :], splb[s_i], True, False,
                             reads=["tri", ("splb", s_i)], writes=[bk(rb)])
                    if acc_i is not None:
                        P.matmul(banks[rb][:, :], ones[:, :], accb[acc_i], False, False,
                                 reads=["ones", ("accb", acc_i)], writes=[bk(rb)])
                    P.matmul(banks[rb][:, :], nkT[:, h, ssl], qT[:, h, tsl], False, True,
                             reads=kq_r, writes=[bk(rb)])
                    a_i = abr.next()
                    if off >= 0:
                        r_i = rbr.next()
                        P.tt("dve", rbm[r_i], banks[rb][:, :], mbig[:, off, :], ALU.add,
                             reads=[bk(rb), "mbig"], writes=[("rbm", r_i)], after=OVK)
                        P.activation("act", ab[a_i], rbm[r_i], AF.Exp, scale=-1.0,
                                     reads=[("rbm", r_i)], writes=[("ab", a_i)], after=OVK)
                    else:
                        P.activation("act", ab[a_i], banks[rb][:, :], AF.Exp, scale=-1.0,
                                     reads=[bk(rb)], writes=[("ab", a_i)], after=OVK)
                    P.matmul(banks[ob][:, :], vtok[:, i, h * 128:(h + 1) * 128], ab[a_i], i == imax, i == 0,
                             reads=[("vtok", i), ("ab", a_i)], writes=[bk(ob)])
                    if i > 0:
                        if acc_i is None:
                            acc_i = acr.next()
                            P.copy("pool", accb[acc_i], splb[s_i], reads=[("splb", s_i)],
                                   writes=[("accb", acc_i)], after=OVK)
                        else:
                            new = acr.next()
                            P.tt("pool", accb[new], accb[acc_i], splb[s_i], ALU.add,
                                 reads=[("accb", acc_i), ("splb", s_i)], writes=[("accb", new)],
                                 after=OVK)
                            acc_i = new
                o_i = osr.next()
                P.copy("dve", osb[o_i], banks[ob][:, :], reads=[bk(ob)], writes=[("osb", o_i)], after=OVK)
                P.dma("sp", o_v[:, h, tsl], osb[o_i], [("osb", o_i)], [("od", h, T)], key=("osb", o_i))
        P.wait_all("sp", [("od", h, T) for h in range(4) for T in range(NB)])
        P.emit()
    return nc


_PROGS = {}


def _prog(name, fn):
    if name not in _PROGS:
        _PROGS[name] = fn()
    return _PROGS[name]


USE_AG = False


def _rows(w, c):
    if not USE_AG:
        return w
    r = w.shape[0] // 8
    return np.ascontiguousarray(w[c * r:(c + 1) * r])


def _run(nc, in_maps):
    res = run_bass_kernel_spmd(nc, in_maps, core_ids=list(range(8)))
    return res.results


def kernel(x, attn_norm_w, ffn_norm_w, gla_w_in, gla_w_gate_up, gla_b_gate, gla_gnorm_w, gla_w_out,
           kv_norm_w, sb_w_kv, sb_w_q, sb_w_out, ffn_w_gate_up, ffn_w_down, final_norm_w):
    f32 = np.float32
    x = np.asarray(x, f32)
    B, S, _ = x.shape
    TQ = S // 4
    w_in = np.asarray(gla_w_in, f32)[0]
    gconst = gla_consts()
    xT = [np.ascontiguousarray(x[b].T) for b in range(B)]
    ims = []
    for c in range(8):
        b, h = c // 4, c % 4
        cols = np.concatenate([np.arange(h * 256, (h + 1) * 256), 1024 + np.arange(h * 256, (h + 1) * 256),
                               2048 + np.arange(h * 512, (h + 1) * 512),
                               4096 + np.arange(h * 512, (h + 1) * 512), 6144 + np.arange(16)])
        wgu = np.concatenate([np.asarray(gla_w_gate_up, f32)[0][:, h * 256:(h + 1) * 256],
                              np.asarray(gla_b_gate, f32)[0][None, h * 256:(h + 1) * 256]], 0)
        im = {"xT": xT[b], "wh": np.ascontiguousarray(w_in[:, cols]),
              "nw": pack_norm_w([np.asarray(attn_norm_w, f32)[0]]),
              "gw": np.ascontiguousarray(np.asarray(gla_gnorm_w, f32)[0].reshape(4, 128).T),
              "wgu": np.ascontiguousarray(wgu)}
        im.update(gconst)
        ims.append(im)
    resA = _run(_prog("A", lambda: build_gla_phase(S)), ims)
    ims = []
    for c in range(8):
        b, j = c // 4, c % 4
        sl = slice(j * TQ, (j + 1) * TQ)
        oT = np.concatenate([np.asarray(resA[b * 4 + h]["ogT"])[:, sl] for h in range(4)], axis=0)
        ims.append({"oT": np.ascontiguousarray(oT), "hT": np.ascontiguousarray(xT[b][:, sl]),
                    "w_out": _rows(np.asarray(gla_w_out, f32)[0], c),
                    "w_gu": _rows(np.asarray(ffn_w_gate_up, f32)[0], c),
                    "w_dn": _rows(np.asarray(ffn_w_down, f32)[0], c),
                    "nw": pack_norm_w([np.asarray(ffn_norm_w, f32)[0], np.asarray(kv_norm_w, f32),
                                       np.asarray(attn_norm_w, f32)[1]])})
    resB = _run(_prog("B", lambda: build_ffn_phase(False, True, TQ, ag=USE_AG)), ims)
    sconst = sb_consts()
    w_kv = np.asarray(sb_w_kv, f32)
    w_q = np.asarray(sb_w_q, f32)[0]
    ims = []
    for c in range(8):
        b, hg = c // 4, c % 4
        hk = np.concatenate([np.asarray(resB[b * 4 + j]["post0"]) for j in range(4)], axis=1)
        hq = np.concatenate([np.asarray(resB[b * 4 + j]["post1"]) for j in range(4)], axis=1)
        im = {"hkT": np.ascontiguousarray(hk), "hqT": np.ascontiguousarray(hq),
              "wk": np.ascontiguousarray(w_kv[:, hg * 512:(hg + 1) * 512]),
              "wv": np.ascontiguousarray(w_kv[:, 2048 + hg * 512:2048 + (hg + 1) * 512]),
              "wq": np.ascontiguousarray(w_q[:, hg * 512:(hg + 1) * 512])}
        im.update(sconst)
        ims.append(im)
    resC = _run(_prog("C", lambda: build_sb_phase(S)), ims)
    ims = []
    for c in range(8):
        b, j = c // 4, c % 4
        sl = slice(j * TQ, (j + 1) * TQ)
        oT = np.concatenate([np.asarray(resC[b * 4 + hg]["oT"])[:, sl] for hg in range(4)], axis=0)
        ims.append({"oT": np.ascontiguousarray(oT), "hT": np.asarray(resB[c]["h2T"]),
                    "w_out": _rows(np.asarray(sb_w_out, f32)[0], c),
                    "w_gu": _rows(np.asarray(ffn_w_gate_up, f32)[1], c),
                    "w_dn": _rows(np.asarray(ffn_w_down, f32)[1], c),
                    "nw": pack_norm_w([np.asarray(ffn_norm_w, f32)[1], np.asarray(final_norm_w, f32)])})
    resD = _run(_prog("D", lambda: build_ffn_phase(True, False, TQ, ag=USE_AG)), ims)
    out = np.empty((B, S, D), f32)
    for c in range(8):
        b, j = c // 4, c % 4
        out[b, j * TQ:(j + 1) * TQ, :] = np.asarray(resD[c]["post0"]).T
    return out
```

```python
import numpy as np
from contextlib import ExitStack
import concourse.bass as bass
import concourse.mybir as mybir
from concourse.bass_utils import run_bass_kernel_spmd

F32 = mybir.dt.float32
BF16 = mybir.dt.bfloat16
AF = mybir.ActivationFunctionType
ALU = mybir.AluOpType

D = 2048
KC = 16
DFF = 5632
JC = 44
EPS = 1e-6
ENGS = ["pe", "act", "dve", "pool", "sp"]
SAME_ENG_SYNC = True


class Op:
    __slots__ = ("eng", "fn", "deps", "needs_inc", "dma_key", "val", "sem", "inc")

    def __init__(self, eng, fn, dma_key):
        self.eng = eng
        self.fn = fn
        self.deps = []
        self.needs_inc = False
        self.dma_key = dma_key
        self.val = 0
        self.sem = None


class Prog:
    def __init__(self, nc, stack):
        self.nc = nc
        self.stack = stack
        self.ops = {e: [] for e in ENGS}
        self.last_w = {}
        self.readers = {}
        self.dma_cnt = {}
        self.dma_sem = {}
        self.eng_sem = {}
        for e in ENGS:
            self.eng_sem[e] = stack.enter_context(nc.semaphore("prog_" + e))
        self.nps = 0

    def sb(self, name, shape, dtype):
        return self.stack.enter_context(self.nc.sbuf_tensor(name, list(shape), dtype))

    def ps(self, name, shape, dtype=F32):
        return self.stack.enter_context(self.nc.psum_tensor(name, list(shape), dtype))

    def add(self, eng, fn, reads=(), writes=(), dma_key=None, after=(), inc=16):
        op = Op(eng, fn, dma_key)
        op.inc = inc
        deps = {}

        def dep(d, kind):
            if d is None or d is op:
                return
            if d.dma_key is None and d.eng == eng:
                if eng == "pe" or kind == "war" or not SAME_ENG_SYNC:
                    return
            deps[id(d)] = d

        for r in reads:
            dep(self.last_w.get(r), "raw")
        for r in writes:
            dep(self.last_w.get(r), "waw")
            for rd in self.readers.get(r, ()):
                dep(rd, "war")
        for r in after:
            dep(self.last_w.get(r), "waw")
            for rd in self.readers.get(r, ()):
                dep(rd, "waw")
        for r in reads:
            self.readers.setdefault(r, []).append(op)
        for r in writes:
            self.last_w[r] = op
            self.readers[r] = []
        op.deps = list(deps.values())
        for d in op.deps:
            if d.dma_key is None:
                d.needs_inc = True
        if dma_key is not None:
            if dma_key not in self.dma_sem:
                self.dma_sem[dma_key] = self.stack.enter_context(
                    self.nc.semaphore("dma_%d" % len(self.dma_sem)))
                self.dma_cnt[dma_key] = 0
            self.dma_cnt[dma_key] += 1
            op.sem = self.dma_sem[dma_key]
            op.val = inc * self.dma_cnt[dma_key]
        else:
            op.sem = self.eng_sem[eng]
        self.ops[eng].append(op)
        return op

    def matmul(self, out, lhsT, rhs, start, stop, reads, writes):
        return self.add("pe", lambda e: e.matmul(out, lhsT, rhs, start=start, stop=stop),
                        reads, writes)

    def dma(self, eng, out, in_, reads, writes, key, after=()):
        return self.add(eng, lambda e: e.dma_start(out=out, in_=in_), reads, writes,
                        dma_key=key, after=after)

    def activation(self, eng, out, in_, func, reads, writes, bias=None, scale=None,
                   accum_out=None, after=()):
        kw = {}
        if bias is not None:
            kw["bias"] = bias
        if scale is not None:
            kw["scale"] = scale
        if accum_out is not None:
            kw["accum_out"] = accum_out
        return self.add(eng, lambda e: e.activation(out=out, in_=in_, func=func, **kw),
                        reads, writes, after=after)

    def tt(self, eng, out, in0, in1, op, reads, writes, after=()):
        return self.add(eng, lambda e: e.tensor_tensor(out=out, in0=in0, in1=in1, op=op),
                        reads, writes, after=after)

    def ts(self, eng, out, in0, s1, s2, op0, op1, reads, writes, after=()):
        if s2 is None:
            return self.add(eng, lambda e: e.tensor_scalar(out=out, in0=in0, scalar1=s1,
                                                           scalar2=None, op0=op0),
                            reads, writes, after=after)
        return self.add(eng, lambda e: e.tensor_scalar(out=out, in0=in0, scalar1=s1, scalar2=s2,
                                                       op0=op0, op1=op1), reads, writes,
                        after=after)

    def stt(self, eng, out, in0, scalar, in1, op0, op1, reads, writes, after=()):
        return self.add(eng, lambda e: e.scalar_tensor_tensor(out=out, in0=in0, scalar=scalar,
                                                              in1=in1, op0=op0, op1=op1),
                        reads, writes, after=after)

    def copy(self, eng, out, in_, reads, writes, after=()):
        if eng == "act":
            return self.add(eng, lambda e: e.activation(out=out, in_=in_, func=AF.Copy), reads,
                            writes, after=after)
        return self.add(eng, lambda e: e.tensor_copy(out=out, in_=in_), reads, writes,
                        after=after)

    def memset(self, eng, ap, val, writes):
        return self.add(eng, lambda e: e.memset(ap, val), (), writes)

    def allgather(self, out_t, in_t, reads, writes, groups=None):
        groups = groups or [list(range(8))]
        self.ncc = getattr(self, "ncc", 0) + 1
        return self.add("pool", lambda e: e.collective_compute(
            "AllGather", ALU.bypass, replica_groups=groups, ins=[in_t.ap().opt()],
            outs=[out_t.ap().opt()]), reads, writes, dma_key=("cc", self.ncc), inc=1)

    def wait_all(self, eng, keys):
        return self.add(eng, None, (), (), after=keys)

    def emit(self):
        nc = self.nc
        for e in ENGS:
            cnt = 0
            for op in self.ops[e]:
                if op.dma_key is None:
                    if op.needs_inc:
                        assert op.fn is not None
                        cnt += 1
                    op.val = cnt

        def body(e):
            def f(eng):
                waited = {}
                for op in self.ops[e]:
                    need = {}
                    for d in op.deps:
                        k = id(d.sem)
                        if k not in need or need[k][1] < d.val:
                            need[k] = (d.sem, d.val)
                    for k, (s, v) in need.items():
                        if waited.get(k, 0) < v:
                            eng.wait_ge(s, v)
                            waited[k] = v
                    if op.fn is not None:
                        inst = op.fn(eng)
                        if op.dma_key is not None and op.inc == 1:
                            inst.then_inc(op.sem)
                        elif op.dma_key is not None:
                            inst.then_inc(op.sem, 16)
                        elif op.needs_inc:
                            inst.then_inc(op.sem, 1)
            return f

        with nc.Block() as block:
            block.tensor(body("pe"))
            block.scalar(body("act"))
            block.vector(body("dve"))
            block.gpsimd(body("pool"))
            block.sync(body("sp"))


class Rot:
    def __init__(self, items):
        self.items = list(items)
        self.i = 0

    def next(self):
        x = self.items[self.i % len(self.items)]
        self.i += 1
        return x


def rmsnorm_stats_to_rstd(P, ss_banks, rstd, T, dim=D, keys=None):
    for hf in range(T // 512):
        sl = slice(hf * 512, (hf + 1) * 512)
        P.activation("act", rstd[:, sl], ss_banks[hf][:, :], AF.Sqrt, scale=1.0 / dim, bias=EPS,
                     reads=[keys[hf] if keys else ("bank", id(ss_banks[hf]))], writes=[("rstd", hf)])
        o_, i_ = rstd[:, sl], rstd[:, sl]
        P.add("dve", lambda e, o_=o_, i_=i_: e.reciprocal(out=o_, in_=i_),
              reads=[("rstd", hf)], writes=[("rstd", hf)])


def build_ffn_phase(final, extra_norms, T=1024, ag=False):
    nc = bass.Bass("TRN2", target_bir_lowering=False)
    NH = T // 512
    oT_d = nc.dram_tensor("oT", [D, T], BF16, kind="ExternalInput").ap()
    hT_d = nc.dram_tensor("hT", [D, T], F32, kind="ExternalInput").ap()
    wsh, wbn, wfl = {}, {}, {}
    for nm, (r, c) in (("w_out", (D, D)), ("w_gu", (D, 2 * DFF)), ("w_dn", (DFF, D))):
        if ag:
            wsh[nm] = nc.dram_tensor(nm, [r // 8, c], F32, kind="ExternalInput")
            wbn[nm] = nc.dram_tensor(nm + "_bn", [r // 8, c], F32)
            wfl[nm] = nc.dram_tensor(nm + "_full", [r, c], F32)
        else:
            wfl[nm] = nc.dram_tensor(nm, [r, c], F32, kind="ExternalInput")
    w_out_d, w_gu_d, w_dn_d = wfl["w_out"].ap(), wfl["w_gu"].ap(), wfl["w_dn"].ap()
    n_post = 2 if extra_norms else (1 if final else 0)
    nw_d = nc.dram_tensor("nw", [128, (1 + n_post) * KC], F32, kind="ExternalInput").ap()
    h1_d = nc.dram_tensor("h1_scr", [D, T], F32, kind="Internal").ap()
    h2_d = nc.dram_tensor("h2T", [D, T], F32, kind="ExternalOutput").ap()
    post_d = []
    for i in range(n_post):
        post_d.append(nc.dram_tensor("post%d" % i, [D, T], BF16 if extra_norms else F32,
                                     kind="ExternalOutput").ap())

    with ExitStack() as stack:
        P = Prog(nc, stack)
        big = P.sb("big", [128, JC * T], BF16)
        act = big[:, :].rearrange("p (j t) -> p j t", j=JC)
        hres = big[:, 0:KC * T * 2].bitcast(F32).rearrange("p (c t) -> p c t", c=KC)
        hn = P.sb("hn", [128, KC, T], BF16)
        NW = 3
        wslot = [P.sb("wslot%d" % i, [128, 11264], BF16) for i in range(NW)]
        wrot = Rot(range(NW))
        rstd = P.sb("rstd", [128, T], F32)
        sq = [P.sb("sq%d" % i, [128, 512], BF16) for i in range(2)]
        sqrot = Rot(range(2))
        sil = [P.sb("sil%d" % i, [128, 512], F32) for i in range(2)]
        silrot = Rot(range(2))
        h1t = [P.sb("h1t%d" % i, [128, 512], F32) for i in range(2)]
        h1rot = Rot(range(2))
        ot = [P.sb("ot%d" % i, [128, 512], F32) for i in range(2)]
        otrot = Rot(range(2))
        nw = P.sb("nwsb", [128, 1 + n_post, KC], F32)
        ones = P.sb("ones", [128, 128], BF16)
        banks = [P.ps("bank%d" % i, [128, 512], F32) for i in range(8)]

        def bk(i):
            return ("bank", id(banks[i]))

        hT_v = hT_d.rearrange("(c p) t -> p c t", p=128)
        oT_v = oT_d.rearrange("(c p) t -> p c t", p=128)
        h1_v = h1_d.rearrange("(c p) t -> p c t", p=128)
        h2_v = h2_d.rearrange("(c p) t -> p c t", p=128)

        for nm in ("w_out", "w_gu", "w_dn"):
            if ag:
                P.dma("sp", wbn[nm].ap(), wsh[nm].ap(), [], [("wbn", nm)], key=("wbn", nm))
        for nm in ("w_out", "w_gu", "w_dn"):
            if ag:
                P.allgather(wfl[nm], wbn[nm], [("wbn", nm)], [("wfull", nm)])
        P.memset("pool", ones[:, :], 1.0, writes=["ones"])
        P.dma("sp", nw[:, :, :], nw_d.rearrange("p (n c) -> p n c", c=KC), [], ["nw"], key="nw")
        for c in range(KC):
            P.dma("sp", hres[:, c, :], hT_v[:, c, :], [], [("h", c)], key=("h", c))
            P.dma("pool", hn[:, c, :], oT_v[:, c, :], [], [("hn", c)], key=("hn", c))

        w_out_v = w_out_d.rearrange("(kc p) n -> p kc n", p=128)
        ssb = [6, 7]
        brot = Rot([(0, 1), (2, 3), (4, 5)])
        for dg in range(8):
            s = wrot.next()
            wt = wslot[s][:, 0:KC * 256].rearrange("p (k n) -> p k n", k=KC)
            P.dma("pool", wt, w_out_v[:, :, dg * 256:(dg + 1) * 256], [("wfull", "w_out")], [("w", s)],
                  key=("w", s))
            for dd in range(2):
                d = dg * 2 + dd
                bset = brot.next()
                for k in range(KC):
                    for hf in range(NH):
                        P.matmul(banks[bset[hf]][:, :], wt[:, k, dd * 128:(dd + 1) * 128],
                                 hn[:, k, hf * 512:(hf + 1) * 512], k == 0, k == KC - 1,
                                 reads=[("w", s), ("hn", k)], writes=[bk(bset[hf])])
                for hf in range(NH):
                    sl = slice(hf * 512, (hf + 1) * 512)
                    P.tt("dve", hres[:, d, sl], banks[bset[hf]][:, :], hres[:, d, sl], ALU.add,
                         reads=[bk(bset[hf]), ("h", d)], writes=[("h", d)])
                    q = sqrot.next()
                    P.activation("act", sq[q][:, :], hres[:, d, sl], AF.Square,
                                 reads=[("h", d)], writes=[("sq", q)])
                    P.matmul(banks[ssb[hf]][:, :], ones[:, :], sq[q][:, :], d == 0, d == KC - 1,
                             reads=["ones", ("sq", q)], writes=[bk(ssb[hf])])
                P.dma("sp", h1_v[:, d, :], hres[:, d, :], [("h", d)], [("h1d", d)], key=("h1d", d))

        rmsnorm_stats_to_rstd(P, [banks[6], banks[7]], rstd, T)
        erot = Rot(["dve"])
        for c in range(KC):
            P.stt(erot.next(), hn[:, c, :], hres[:, c, :], nw[:, 0, c:c + 1], rstd[:, :],
                  ALU.mult, ALU.mult,
                  reads=[("h", c), "nw", ("rstd", 0), ("rstd", 1)], writes=[("hn", c)])

        w_gu_v = w_gu_d.rearrange("(kc p) n -> p kc n", p=128)
        b4rot = Rot([(0, 1, 2, 3), (4, 5, 6, 7)])
        for jp in range(JC // 2):
            s = wrot.next()
            wg = wslot[s][:, 0:KC * 256].rearrange("p (k n) -> p k n", k=KC)
            wu = wslot[s][:, KC * 256:2 * KC * 256].rearrange("p (k n) -> p k n", k=KC)
            P.dma("pool", wg, w_gu_v[:, :, jp * 256:(jp + 1) * 256], [("wfull", "w_gu")], [("w", s)],
                  key=("w", s))
            P.dma("pool", wu, w_gu_v[:, :, DFF + jp * 256:DFF + (jp + 1) * 256], [("wfull", "w_gu")],
                  [("w", s, 1)], key=("w", s, 1))
            for jj in range(2):
                j = jp * 2 + jj
                bset = b4rot.next()
                for gi, wv in enumerate((wg, wu)):
                    for k in range(KC):
                        for hf in range(NH):
                            b = bset[gi * 2 + hf]
                            P.matmul(banks[b][:, :], wv[:, k, jj * 128:(jj + 1) * 128],
                                     hn[:, k, hf * 512:(hf + 1) * 512], k == 0, k == KC - 1,
                                     reads=[("w", s), ("w", s, 1), ("hn", k)], writes=[bk(b)])
                for hf in range(NH):
                    sl = slice(hf * 512, (hf + 1) * 512)
                    si = silrot.next()
                    P.activation("act", sil[si][:, :], banks[bset[hf]][:, :], AF.Silu,
                                 reads=[bk(bset[hf])], writes=[("sil", si)])
                    P.tt("dve", act[:, j, sl], banks[bset[2 + hf]][:, :], sil[si][:, :], ALU.mult,
                         reads=[bk(bset[2 + hf]), ("sil", si)], writes=[("act", j)],
                         after=[("h", j // 2)])

        w_dn_v = w_dn_d.rearrange("(jc p) n -> p jc n", p=128)
        ssb = [6, 7]
        brot = Rot([(0, 1), (2, 3), (4, 5)])
        for dg in range(8):
            s = wrot.next()
            wt = wslot[s][:, 0:JC * 256].rearrange("p (k n) -> p k n", k=JC)
            P.dma("pool", wt[:, 0:22, :], w_dn_v[:, 0:22, dg * 256:(dg + 1) * 256], [("wfull", "w_dn")],
                  [("w", s)], key=("w", s))
            P.dma("pool", wt[:, 22:44, :], w_dn_v[:, 22:44, dg * 256:(dg + 1) * 256], [("wfull", "w_dn")],
                  [("w", s, 1)], key=("w", s, 1))
            for dd in range(2):
                d = dg * 2 + dd
                bset = brot.next()
                for j in range(JC):
                    for hf in range(NH):
                        P.matmul(banks[bset[hf]][:, :], wt[:, j, dd * 128:(dd + 1) * 128],
                                 act[:, j, hf * 512:(hf + 1) * 512], j == 0, j == JC - 1,
                                 reads=[("w", s), ("w", s, 1), ("act", j)], writes=[bk(bset[hf])])
                for hf in range(NH):
                    sl = slice(hf * 512, (hf + 1) * 512)
                    hi = h1rot.next()
                    P.dma("sp", h1t[hi][:, :], h1_v[:, d, sl], [("h1d", d)], [("h1t", hi)],
                          key=("h1t", hi))
                    oi = otrot.next()
                    P.tt("dve", ot[oi][:, :], banks[bset[hf]][:, :], h1t[hi][:, :], ALU.add,
                         reads=[bk(bset[hf]), ("h1t", hi)], writes=[("ot", oi)])
                    P.dma("sp", h2_v[:, d, sl], ot[oi][:, :], [("ot", oi)], [("h2d", d, hf)],
                          key=("ot", oi))
                    if n_post:
                        q = sqrot.next()
                        P.activation("act", sq[q][:, :], ot[oi][:, :], AF.Square,
                                     reads=[("ot", oi)], writes=[("sq", q)])
                        P.matmul(banks[ssb[hf]][:, :], ones[:, :], sq[q][:, :], d == 0, d == KC - 1,
                                 reads=["ones", ("sq", q)], writes=[bk(ssb[hf])])

        if n_post:
            rmsnorm_stats_to_rstd(P, [banks[6], banks[7]], rstd, T)
            pbuf = [big[:, i * 4 * T:(i * 4 + 2) * T].bitcast(F32) for i in range(4)]
            if extra_norms:
                obuf = [big[:, (i * 4 + 2) * T:(i * 4 + 3) * T] for i in range(4)]
            else:
                obuf = [big[:, (i * 4 + 2) * T:(i * 4 + 4) * T].bitcast(F32) for i in range(4)]
            prot = Rot(range(4))
            orot = Rot(range(4))
            allact = [("act", j) for j in range(JC)]
            for d in range(KC):
                pi = prot.next()
                P.dma("sp", pbuf[pi], h2_v[:, d, :], [("h2d", d, 0), ("h2d", d, 1)], [("pbuf", pi)],
                      key=("pbuf", pi), after=allact)
                for n in range(n_post):
                    oi = orot.next()
                    P.stt(erot.next(), obuf[oi], pbuf[pi], nw[:, 1 + n, d:d + 1], rstd[:, :],
                          ALU.mult, ALU.mult,
                          reads=[("pbuf", pi), "nw", ("rstd", 0), ("rstd", 1)], writes=[("obuf", oi)],
                          after=allact)
                    P.dma("sp", post_d[n].rearrange("(c p) t -> p c t", p=128)[:, d, :], obuf[oi],
                          [("obuf", oi)], [("postd", n, d)], key=("obuf", oi))
        outs = [("h2d", d, hf) for d in range(KC) for hf in range(NH)]
        outs += [("postd", n, d) for n in range(n_post) for d in range(KC)]
        P.wait_all("sp", outs)
        P.emit()
    return nc


def pack_norm_w(ws):
    arr = np.stack([np.asarray(w, np.float32).reshape(KC, 128).T for w in ws], axis=1)
    return np.ascontiguousarray(arr.reshape(128, -1))


GDK, GDV, GC = 256, 512, 64
WHC = 2 * GDK + 2 * GDV + 128


def gla_consts():
    j = np.arange(128)[:, None]
    i = np.arange(128)[None, :]
    same = (j // 64) == (i // 64)
    msuf = (same & (j > i)).astype(np.float32)
    cm = (same & (j <= i)).astype(np.float32)
    rm = np.ones((128, 512), np.float32)
    rm[:, ::64] = 0.0
    gi = np.zeros((128, 512), np.float32)
    gi[16, :] = 1.0
    hm = np.stack([(np.arange(128) < 64), (np.arange(128) >= 64)], 1).astype(np.float32)
    return {"msuf": msuf, "cmask": np.ascontiguousarray(cm), "rmask": rm, "glinit": gi,
            "hmask": np.ascontiguousarray(hm)}


def build_gla_phase(S=4096):
    nc = bass.Bass("TRN2", target_bir_lowering=False)
    NB = S // 512
    xT_d = nc.dram_tensor("xT", [D, S], F32, kind="ExternalInput").ap()
    wh_d = nc.dram_tensor("wh", [D, WHC], F32, kind="ExternalInput").ap()
    nw_d = nc.dram_tensor("nw", [128, KC], F32, kind="ExternalInput").ap()
    gw_d = nc.dram_tensor("gw", [128, 4], F32, kind="ExternalInput").ap()
    wgu_d = nc.dram_tensor("wgu", [17, GDK], F32, kind="ExternalInput").ap()
    msuf_d = nc.dram_tensor("msuf", [128, 128], F32, kind="ExternalInput").ap()
    cmask_d = nc.dram_tensor("cmask", [128, 128], F32, kind="ExternalInput").ap()
    glinit_d = nc.dram_tensor("glinit", [128, 512], F32, kind="ExternalInput").ap()
    hmask_d = nc.dram_tensor("hmask", [128, 2], F32, kind="ExternalInput").ap()
    rmask_d = nc.dram_tensor("rmask", [128, 512], F32, kind="ExternalInput").ap()
    og_d = nc.dram_tensor("ogT", [GDV, S], BF16, kind="ExternalOutput").ap()

    with ExitStack() as stack:
        P = Prog(nc, stack)
        W = P.sb("W", [128, KC, WHC], BF16)
        xt = [P.sb("xt%d" % i, [128, KC, 512], F32) for i in range(1)]
        sqall = P.sb("sqall", [128, KC * 512], BF16)
        aT = P.sb("aT", [128, KC, 512], BF16)
        rstd = P.sb("rstd", [128, 512], F32)
        nw = P.sb("nwsb", [128, KC], F32)
        gw = P.sb("gwsb", [128, 4], F32)
        wgu = P.sb("wgusb", [128, GDK], F32)
        hmask = P.sb("hmasksb", [128, 2], F32)
        msuf = P.sb("msufsb", [128, 128], F32)
        cmask = P.sb("cmasksb", [128, 128], F32)
        rmask = P.sb("rmasksb", [128, 512], F32)
        ones = P.sb("ones", [128, 128], BF16)
        glaug = P.sb("glaug", [128, 512], F32)
        ex = [P.sb("ex%d" % i, [128, 512], F32) for i in range(2)]
        sp = ex
        cs = [P.sb("cs%d" % i, [128, 512], F32) for i in range(2)]
        E1 = [P.sb("E1_%d" % i, [128, 512], F32) for i in range(2)]
        E2 = [P.sb("E2_%d" % i, [128, 512], F32) for i in range(2)]
        qe = [P.sb("qe%d" % i, [128, 512], BF16) for i in range(2)]
        ke = [P.sb("ke%d" % i, [128, 512], BF16) for i in range(2)]
        rsil = [P.sb("rsil%d" % i, [128, 512], F32) for i in range(4)]
        vtok = [P.sb("vtok%d" % i, [128, GDV], BF16) for i in range(4)]
        kd = [[P.sb("kd%d_%d" % (i, hh), [128, GDK], BF16) for hh in range(2)] for i in range(4)]
        spt = [P.sb("spt%d" % i, [128, GDK], F32) for i in range(4)]
        dk_ = [P.sb("dkt%d" % i, [128, GDK], F32) for i in range(2)]
        scsb = P.sb("scsb", [128, 128], BF16)
        S32 = [P.sb("S32_%d" % i, [128, GDV], F32) for i in range(2)]
        Sbf = [P.sb("Sbf_%d" % i, [128, GDV], BF16) for i in range(2)]
        ogf = [P.sb("ogf%d" % i, [128, 512], F32) for i in range(2)]
        ogb = [P.sb("ogb%d" % i, [128, 512], BF16) for i in range(2)]
        banks = [P.ps("bank%d" % i, [128, 512], F32) for i in range(8)]
        grot = Rot([0, 1, 2])
        SSB = 3

        def bk(i):
            return ("bank", i)

        P.memset("pool", ones[:, :], 1.0, writes=["ones"])
        P.dma("sp", glaug[:, :], glinit_d, [], ["glaug"], key="glaug")
        P.memset("pool", wgu[:, :], 0.0, writes=["wgu"])
        P.dma("sp", hmask[:, :], hmask_d, [], ["hmask"], key="hmask")
        for i in range(2):
            P.memset("pool", S32[i][:, :], 0.0, writes=[("S32", i)])
            P.memset("pool", Sbf[i][:, :], 0.0, writes=[("Sbf", i)])
        P.dma("sp", nw[:, :], nw_d, [], ["nw"], key="nw")
        P.dma("sp", gw[:, :], gw_d, [], ["gw"], key="gw")
        P.dma("sp", wgu[0:17, :], wgu_d, [], ["wgu"], key="wgu")
        P.dma("sp", msuf[:, :], msuf_d, [], ["msuf"], key="msuf")
        P.dma("sp", cmask[:, :], cmask_d, [], ["cmask"], key="cmask")
        P.dma("sp", rmask[:, :], rmask_d, [], ["rmask"], key="rmask")
        wh_v = wh_d.rearrange("(kc p) n -> p kc n", p=128)
        for k in range(KC):
            P.dma("pool", W[:, k, :], wh_v[:, k, :], [], [("W", k)], key=("W", k))
        xT_v = xT_d.rearrange("(c p) t -> p c t", p=128)
        og_v = og_d.rearrange("(c p) t -> p c t", p=128)
        Wk = [("W", k) for k in range(KC)]
        CQ, CK, CV, CR, CG = 0, GDK, 2 * GDK, 2 * GDK + GDV, 2 * GDK + 2 * GDV

        def load_x(n):
            xb = 0
            for c in range(KC):
                P.dma("sp", xt[xb][:, c, :], xT_v[:, c, n * 512:(n + 1) * 512], [],
                      [("xt", xb, c)], key=("xt", xb, c))

        def fm_proj(col0, M):
            b = grot.next()
            for k in range(KC):
                P.matmul(banks[b][0:M, :], W[:, k, col0:col0 + M], aT[:, k, :], k == 0, k == KC - 1,
                         reads=[("W", k), ("aT", k)], writes=[bk(b)])
            return b

        load_x(0)
        for n in range(NB):
            xb = 0
            xkeys = [("xt", xb, c) for c in range(KC)]
            P.activation("act", sqall[:, :], xt[xb][:, :, :].rearrange("p c t -> p (c t)"), AF.Square,
                         reads=xkeys, writes=["sqall"])
            for c in range(KC):
                P.matmul(banks[SSB][:, :], ones[:, :], sqall[:, c * 512:(c + 1) * 512], c == 0,
                         c == KC - 1, reads=["ones", "sqall"], writes=[bk(SSB)])
            rmsnorm_stats_to_rstd(P, [banks[SSB]], rstd, 512, keys=[bk(SSB)])
            for c in range(KC):
                P.stt("dve", aT[:, c, :], xt[xb][:, c, :], nw[:, c:c + 1], rstd[:, :], ALU.mult, ALU.mult,
                      reads=[("xt", xb, c), "nw", ("rstd", 0)], writes=[("aT", c)])
            if n + 1 < NB:
                load_x(n + 1)
            b = fm_proj(CG, 128)
            P.copy("dve", glaug[0:16, :], banks[b][0:16, :], reads=[bk(b)], writes=["glaug"])
            for dkc in range(2):
                b = grot.next()
                P.matmul(banks[b][:, :], wgu[:, dkc * 128:(dkc + 1) * 128], glaug[:, :], True, True,
                         reads=["wgu", "glaug"], writes=[bk(b)])
                P.activation("act", ex[dkc][:, :], banks[b][:, :], AF.Exp, scale=-1.0,
                             reads=[bk(b)], writes=[("ex", dkc)])
            for dkc in range(2):
                P.activation("act", sp[dkc][:, :], ex[dkc][:, :], AF.Ln, bias=1.0,
                             reads=[("ex", dkc)], writes=[("ex", dkc), ("sp", dkc)])
                o_, m_, s_ = cs[dkc][:, :], rmask[:, :], sp[dkc][:, :]
                P.add("dve", lambda e, o_=o_, m_=m_, s_=s_: e.tensor_tensor_scan(
                    out=o_, data0=m_, data1=s_, initial=0.0, op0=ALU.mult, op1=ALU.add),
                    reads=["rmask", ("sp", dkc)], writes=[("cs", dkc)])
            for dkc in range(2):
                P.activation("act", E1[dkc][:, :], cs[dkc][:, :], AF.Exp, scale=-1.0 / 16,
                             reads=[("cs", dkc)], writes=[("E1", dkc)])
                P.activation("act", E2[dkc][:, :], cs[dkc][:, :], AF.Exp, scale=1.0 / 16,
                             reads=[("cs", dkc)], writes=[("E2", dkc)])
            for dkc in range(2):
                b = fm_proj(CQ + dkc * 128, 128)
                P.stt("dve", qe[dkc][:, :], banks[b][:, :], float(GDK) ** -0.5, E1[dkc][:, :],
                      ALU.mult, ALU.mult, reads=[bk(b), ("E1", dkc)], writes=[("qe", dkc)])
            for dkc in range(2):
                b = fm_proj(CK + dkc * 128, 128)
                P.tt("dve", ke[dkc][:, :], banks[b][:, :], E2[dkc][:, :], ALU.mult,
                     reads=[bk(b), ("E2", dkc)], writes=[("ke", dkc)])
            for tt in range(4):
                tsl = slice(tt * 128, (tt + 1) * 128)
                b = grot.next()
                for k in range(KC):
                    P.matmul(banks[b][:, :], aT[:, k, tsl], W[:, k, CV:CV + GDV], k == 0, k == KC - 1,
                             reads=[("W", k), ("aT", k)], writes=[bk(b)])
                P.copy("act", vtok[tt][:, :], banks[b][:, :], reads=[bk(b)], writes=[("vtok", tt)])
                bg = grot.next()
                P.matmul(banks[bg][:, 0:GDK], glaug[:, tsl], wgu[:, :], True, True,
                         reads=["wgu", "glaug"], writes=[bk(bg)])
                P.activation("act", spt[tt][:, :], banks[bg][:, 0:GDK], AF.Exp, scale=-1.0,
                             reads=[bk(bg)], writes=[("spt", tt)])
                P.activation("act", spt[tt][:, :], spt[tt][:, :], AF.Ln, bias=1.0,
                             reads=[("spt", tt)], writes=[("spt", tt)])
            for tt in range(4):
                tsl = slice(tt * 128, (tt + 1) * 128)
                bs = grot.next()
                P.matmul(banks[bs][:, 0:GDK], msuf[:, :], spt[tt][:, :], True, True,
                         reads=["msuf", ("spt", tt)], writes=[bk(bs)])
                d_ = dk_[tt % 2]
                P.activation("act", d_[:, :], banks[bs][:, 0:GDK], AF.Exp, scale=-1.0 / 16,
                             reads=[bk(bs)], writes=[("dkt", tt % 2)])
                b = grot.next()
                for k in range(KC):
                    P.matmul(banks[b][:, 0:GDK], aT[:, k, tsl], W[:, k, CK:CK + GDK], k == 0, k == KC - 1,
                             reads=[("W", k), ("aT", k)], writes=[bk(b)])
                for hh in range(2):
                    P.stt("dve", kd[tt][hh][:, :], banks[b][:, 0:GDK], hmask[:, hh:hh + 1], d_[:, :],
                          ALU.mult, ALU.mult, reads=[bk(b), ("dkt", tt % 2), "hmask"],
                          writes=[("kd", tt, hh)])
            for dvc in range(4):
                b = fm_proj(CR + dvc * 128, 128)
                P.activation("act", rsil[dvc][:, :], banks[b][:, :], AF.Silu,
                             reads=[bk(b)], writes=[("rsil", dvc)])
            for tt in range(4):
                cols2 = slice(tt * 128, (tt + 1) * 128)
                for dkc in range(2):
                    P.matmul(banks[SSB][:, 0:128], ke[dkc][:, cols2], qe[dkc][:, cols2], dkc == 0, dkc == 1,
                             reads=[("ke", dkc), ("qe", dkc)], writes=[bk(SSB)])
                P.tt("dve", scsb[:, :], banks[SSB][:, 0:128], cmask[:, :], ALU.mult,
                     reads=[bk(SSB), "cmask"], writes=["scsb"])
                for dvc in range(4):
                    dsl = slice(dvc * 128, (dvc + 1) * 128)
                    P.matmul(banks[4 + dvc][:, cols2], vtok[tt][:, dsl], scsb[:, :], True, False,
                             reads=[("vtok", tt), "scsb"], writes=[("ob", dvc)])
                for hh in range(2):
                    c = tt * 2 + hh
                    cols = slice(c * 64, (c + 1) * 64)
                    for dvc in range(4):
                        dsl = slice(dvc * 128, (dvc + 1) * 128)
                        for dkc in range(2):
                            P.matmul(banks[4 + dvc][:, cols], Sbf[dkc][:, dsl], qe[dkc][:, cols], False,
                                     hh == 1 and dkc == 1,
                                     reads=[("Sbf", dkc), ("qe", dkc)], writes=[("ob", dvc)])
                    for dkc in range(2):
                        b = grot.next()
                        P.matmul(banks[b][:, :], kd[tt][hh][:, dkc * 128:(dkc + 1) * 128], vtok[tt][:, :],
                                 True, True, reads=[("kd", tt, hh), ("vtok", tt)], writes=[bk(b)])
                        lc = c * 64 + 63
                        P.stt("dve", S32[dkc][:, :], S32[dkc][:, :], E1[dkc][:, lc:lc + 1], banks[b][:, :],
                              ALU.mult, ALU.add, reads=[("S32", dkc), ("E1", dkc), bk(b)],
                              writes=[("S32", dkc)])
                        P.copy("pool", Sbf[dkc][:, :], S32[dkc][:, :], reads=[("S32", dkc)],
                               writes=[("Sbf", dkc)])
            for dvc in range(4):
                P.activation("act", sqall[:, dvc * 512:(dvc + 1) * 512], banks[4 + dvc][:, :], AF.Square,
                             reads=[("ob", dvc)], writes=["sqall"])
            for dvc in range(4):
                P.matmul(banks[SSB][:, :], ones[:, :], sqall[:, dvc * 512:(dvc + 1) * 512], dvc == 0,
                         dvc == 3, reads=["ones", "sqall"], writes=[bk(SSB)])
            rmsnorm_stats_to_rstd(P, [banks[SSB]], rstd, 512, dim=GDV, keys=[bk(SSB)])
            for dvc in range(4):
                P.stt("dve", ogf[dvc % 2][:, :], banks[4 + dvc][:, :], gw[:, dvc:dvc + 1], rstd[:, :],
                      ALU.mult, ALU.mult, reads=[("ob", dvc), "gw", ("rstd", 0)],
                      writes=[("ogf", dvc % 2)])
                P.tt("pool", ogb[dvc % 2][:, :], ogf[dvc % 2][:, :], rsil[dvc][:, :], ALU.mult,
                     reads=[("ogf", dvc % 2), ("rsil", dvc)], writes=[("ogb", dvc % 2)])
                P.dma("sp", og_v[:, dvc, n * 512:(n + 1) * 512], ogb[dvc % 2][:, :],
                      [("ogb", dvc % 2)], [("ogd", dvc, n)], key=("ogb", dvc % 2))
        P.wait_all("sp", [("ogd", dvc, n) for dvc in range(4) for n in range(NB)])
        P.emit()
    return nc


HD = 128
BIG = 30000.0


def sb_consts():
    p = np.arange(128)[:, None]
    f = np.arange(512)[None, :]
    m01 = np.stack([(p + 128 * off < f) for off in range(4)]).astype(np.float32)
    mbig = (1.0 - m01) * BIG
    tri = (np.arange(128)[:, None] >= np.arange(128)[None, :]).astype(np.float32)
    return {"m01": np.ascontiguousarray(m01.transpose(1, 0, 2).reshape(128, 4 * 512)),
            "mbig": np.ascontiguousarray(mbig.transpose(1, 0, 2).reshape(128, 4 * 512)),
            "tri": tri}


def build_sb_phase(S=4096):
    nc = bass.Bass("TRN2", target_bir_lowering=False)
    NB = S // 512
    NS = S // 128
    hk_d = nc.dram_tensor("hkT", [D, S], BF16, kind="ExternalInput").ap()
    hq_d = nc.dram_tensor("hqT", [D, S], BF16, kind="ExternalInput").ap()
    wk_d = nc.dram_tensor("wk", [D, 512], F32, kind="ExternalInput").ap()
    wv_d = nc.dram_tensor("wv", [D, 512], F32, kind="ExternalInput").ap()
    wq_d = nc.dram_tensor("wq", [D, 512], F32, kind="ExternalInput").ap()
    m01_d = nc.dram_tensor("m01", [128, 4 * 512], F32, kind="ExternalInput").ap()
    mbig_d = nc.dram_tensor("mbig", [128, 4 * 512], F32, kind="ExternalInput").ap()
    tri_d = nc.dram_tensor("tri", [128, 128], F32, kind="ExternalInput").ap()
    o_d = nc.dram_tensor("oT", [512, S], BF16, kind="ExternalOutput").ap()

    with ExitStack() as stack:
        P = Prog(nc, stack)
        nkT = P.sb("nkT", [128, 4, S], BF16)
        qT = P.sb("qT", [128, 4, S], BF16)
        vtok = P.sb("vtok", [128, NS, 512], BF16)
        ov = P.sb("ov", [128, 40960], BF16)
        wk = ov[:, 0:8192].rearrange("p (k n) -> p k n", k=KC)
        wv = ov[:, 8192:16384].rearrange("p (k n) -> p k n", k=KC)
        wq = ov[:, 16384:24576].rearrange("p (k n) -> p k n", k=KC)
        hs = [ov[:, 24576 + i * 8192:24576 + (i + 1) * 8192].rearrange("p (k n) -> p k n", k=KC)
              for i in range(2)]
        OVK = ["wk", "wv", "wq", ("hs", 0), ("hs", 1)]
        ef = [ov[:, i * 1024:(i + 1) * 1024].bitcast(F32) for i in range(2)]
        rbm = [ov[:, 2048 + i * 1024:2048 + (i + 1) * 1024].bitcast(F32) for i in range(2)]
        splf = [ov[:, 4096 + i * 1024:4096 + (i + 1) * 1024].bitcast(F32) for i in range(2)]
        splb = [ov[:, 6144 + i * 512:6144 + (i + 1) * 512] for i in range(3)]
        accb = [ov[:, 7680 + i * 512:7680 + (i + 1) * 512] for i in range(2)]
        ab = [ov[:, 8704 + i * 512:8704 + (i + 1) * 512] for i in range(3)]
        osb = [ov[:, 10240 + i * 512:10240 + (i + 1) * 512] for i in range(2)]
        m01 = P.sb("m01sb", [128, 4, 512], F32)
        mbig = P.sb("mbigsb", [128, 4, 512], F32)
        trif = P.sb("trif", [128, 128], F32)
        tri = P.sb("trib", [128, 128], BF16)
        ones = P.sb("ones", [128, 128], BF16)
        banks = [P.ps("bank%d" % i, [128, 512], F32) for i in range(8)]

        def bk(i):
            return ("bank", i)

        P.memset("pool", ones[:, :], 1.0, writes=["ones"])
        P.dma("sp", m01[:, :, :], m01_d.rearrange("p (o f) -> p o f", o=4), [], ["m01"], key="m01")
        P.dma("sp", mbig[:, :, :], mbig_d.rearrange("p (o f) -> p o f", o=4), [], ["mbig"], key="mbig")
        P.dma("sp", trif[:, :], tri_d, [], ["trif"], key="trif")
        P.copy("dve", tri[:, :], trif[:, :], reads=["trif"], writes=["tri"])
        for nm, wt, wd in (("wk", wk, wk_d), ("wv", wv, wv_d), ("wq", wq, wq_d)):
            P.dma("pool", wt, wd.rearrange("(kc p) n -> p kc n", p=128), [], [nm], key=nm)
        hk_v = hk_d.rearrange("(c p) t -> p c t", p=128)
        hq_v = hq_d.rearrange("(c p) t -> p c t", p=128)
        o_v = o_d.rearrange("(h p) t -> p h t", p=128)

        prot = Rot([0, 1, 2, 3])
        erot = Rot(["act", "dve"])
        for n in range(NB):
            nsl = slice(n * 512, (n + 1) * 512)
            P.dma("sp", hs[0], hk_v[:, :, nsl], [], [("hs", 0)], key=("hs", 0))
            P.dma("sp", hs[1], hq_v[:, :, nsl], [], [("hs", 1)], key=("hs", 1))
            for h in range(4):
                b = prot.next()
                for k in range(KC):
                    P.matmul(banks[b][:, :], wk[:, k, h * 128:(h + 1) * 128], hs[0][:, k, :], k == 0,
                             k == KC - 1, reads=["wk", ("hs", 0)], writes=[bk(b)])
                P.activation("act", nkT[:, h, nsl], banks[b][:, :], AF.Copy, scale=-1.0,
                             reads=[bk(b)], writes=[("nkT", h, n)])
            for st in range(4):
                b = prot.next()
                for k in range(KC):
                    P.matmul(banks[b][:, :], hs[0][:, k, st * 128:(st + 1) * 128], wv[:, k, :], k == 0,
                             k == KC - 1, reads=["wv", ("hs", 0)], writes=[bk(b)])
                P.copy("dve", vtok[:, n * 4 + st, :], banks[b][:, :], reads=[bk(b)],
                       writes=[("vtok", n * 4 + st)])
            for h in range(4):
                b = prot.next()
                for k in range(KC):
                    P.matmul(banks[b][:, :], wq[:, k, h * 128:(h + 1) * 128], hs[1][:, k, :], k == 0,
                             k == KC - 1, reads=["wq", ("hs", 1)], writes=[bk(b)])
                P.ts("dve", qT[:, h, nsl], banks[b][:, :], float(HD) ** -0.5, None, ALU.mult, None,
                     reads=[bk(b)], writes=[("qT", h, n)])

        zrot = Rot([0, 1])
        rrot = Rot([2, 3])
        orot = Rot([4, 5])
        efr, rbr, sfr, sbr, acr, abr, osr = (Rot(range(2)), Rot(range(2)), Rot(range(2)),
                                             Rot(range(3)), Rot(range(2)), Rot(range(3)),
                                             Rot(range(2)))
        for h in range(4):
            for T in range(NB):
                tsl = slice(T * 512, (T + 1) * 512)
                ob = orot.next()
                imax = 4 * T + 3
                acc_i = None
                for i in range(imax, -1, -1):
                    ssl = slice(i * 128, (i + 1) * 128)
                    off = i - 4 * T
                    kq_r = [("nkT", h, i // 4), ("qT", h, T)]
                    zb = zrot.next()
                    P.matmul(banks[zb][:, :], nkT[:, h, ssl], qT[:, h, tsl], True, True,
                             reads=kq_r, writes=[bk(zb)])
                    e_i = efr.next()
                    P.activation("act", ef[e_i], banks[zb][:, :], AF.Exp, scale=-1.0,
                                 reads=[bk(zb)], writes=[("ef", e_i)], after=OVK)
                    s_i = sbr.next()
                    if off >= 0:
                        f_i = sfr.next()
                        P.activation("act", splf[f_i], ef[e_i], AF.Ln, bias=1.0,
                                     reads=[("ef", e_i)], writes=[("splf", f_i)], after=OVK)
                        P.tt("pool", splb[s_i], splf[f_i], m01[:, off, :], ALU.mult,
                             reads=[("splf", f_i), "m01"], writes=[("splb", s_i)], after=OVK)
                    else:
                        P.activation("act", splb[s_i], ef[e_i], AF.Ln, bias=1.0,
                                     reads=[("ef", e_i)], writes=[("splb", s_i)], after=OVK)
                    rb = rrot.next()
                    P.matmul(banks[rb][:, :], tri[:, :], splb[s_i], True, False,
                             reads=["tri", ("splb", s_i)], writes=[bk(rb)])
                    if acc_i is not None:
                        P.matmul(banks[rb][:, :], ones[:, :], accb[acc_i], False, False,
                                 reads=["ones", ("accb", acc_i)], writes=[bk(rb)])
                    P.matmul(banks[rb][:, :], nkT[:, h, ssl], qT[:, h, tsl], False, True,
                             reads=kq_r, writes=[bk(rb)])
                    a_i = abr.next()
                    if off >= 0:
                        r_i = rbr.next()
                        P.tt("dve", rbm[r_i], banks[rb][:, :], mbig[:, off, :], ALU.add,
                             reads=[bk(rb), "mbig"], writes=[("rbm", r_i)], after=OVK)
                        P.activation("act", ab[a_i], rbm[r_i], AF.Exp, scale=-1.0,
                                     reads=[("rbm", r_i)], writes=[("ab", a_i)], after=OVK)
                    else:
                        P.activation("act", ab[a_i], banks[rb][:, :], AF.Exp, scale=-1.0,
                                     reads=[bk(rb)], writes=[("ab", a_i)], after=OVK)
                    P.matmul(banks[ob][:, :], vtok[:, i, h * 128:(h + 1) * 128], ab[a_i], i == imax, i == 0,
                             reads=[("vtok", i), ("ab", a_i)], writes=[bk(ob)])
                    if i > 0:
                        if acc_i is None:
                            acc_i = acr.next()
                            P.copy("pool", accb[acc_i], splb[s_i], reads=[("splb", s_i)],
                                   writes=[("accb", acc_i)], after=OVK)
                        else:
                            new = acr.next()
                            P.tt("pool", accb[new], accb[acc_i], splb[s_i], ALU.add,
                                 reads=[("accb", acc_i), ("splb", s_i)], writes=[("accb", new)],
                                 after=OVK)
                            acc_i = new
                o_i = osr.next()
                P.copy("dve", osb[o_i], banks[ob][:, :], reads=[bk(ob)], writes=[("osb", o_i)], after=OVK)
                P.dma("sp", o_v[:, h, tsl], osb[o_i], [("osb", o_i)], [("od", h, T)], key=("osb", o_i))
        P.wait_all("sp", [("od", h, T) for h in range(4) for T in range(NB)])
        P.emit()
    return nc


_PROGS = {}


def _prog(name, fn):
    if name not in _PROGS:
        _PROGS[name] = fn()
    return _PROGS[name]


USE_AG = False


def _rows(w, c):
    if not USE_AG:
        return w
    r = w.shape[0] // 8
    return np.ascontiguousarray(w[c * r:(c + 1) * r])


def _run(nc, in_maps):
    res = run_bass_kernel_spmd(nc, in_maps, core_ids=list(range(8)))
    return res.results


def kernel(x, attn_norm_w, ffn_norm_w, gla_w_in, gla_w_gate_up, gla_b_gate, gla_gnorm_w, gla_w_out,
           kv_norm_w, sb_w_kv, sb_w_q, sb_w_out, ffn_w_gate_up, ffn_w_down, final_norm_w):
    f32 = np.float32
    x = np.asarray(x, f32)
    B, S, _ = x.shape
    TQ = S // 4
    w_in = np.asarray(gla_w_in, f32)[0]
    gconst = gla_consts()
    xT = [np.ascontiguousarray(x[b].T) for b in range(B)]
    ims = []
    for c in range(8):
        b, h = c // 4, c % 4
        cols = np.concatenate([np.arange(h * 256, (h + 1) * 256), 1024 + np.arange(h * 256, (h + 1) * 256),
                               2048 + np.arange(h * 512, (h + 1) * 512),
                               4096 + np.arange(h * 512, (h + 1) * 512), 6144 + np.arange(16),
                               np.arange(h * 256, h * 256 + 112)])
        wgu = np.concatenate([np.asarray(gla_w_gate_up, f32)[0][:, h * 256:(h + 1) * 256],
                              np.asarray(gla_b_gate, f32)[0][None, h * 256:(h + 1) * 256]], 0)
        im = {"xT": xT[b], "wh": np.ascontiguousarray(w_in[:, cols]),
              "nw": pack_norm_w([np.asarray(attn_norm_w, f32)[0]]),
              "gw": np.ascontiguousarray(np.asarray(gla_gnorm_w, f32)[0].reshape(4, 128).T),
              "wgu": np.ascontiguousarray(wgu)}
        im.update(gconst)
        ims.append(im)
    resA = _run(_prog("A", lambda: build_gla_phase(S)), ims)
    ims = []
    for c in range(8):
        b, j = c // 4, c % 4
        sl = slice(j * TQ, (j + 1) * TQ)
        oT = np.concatenate([np.asarray(resA[b * 4 + h]["ogT"])[:, sl] for h in range(4)], axis=0)
        ims.append({"oT": np.ascontiguousarray(oT), "hT": np.ascontiguousarray(xT[b][:, sl]),
                    "w_out": _rows(np.asarray(gla_w_out, f32)[0], c),
                    "w_gu": _rows(np.asarray(ffn_w_gate_up, f32)[0], c),
                    "w_dn": _rows(np.asarray(ffn_w_down, f32)[0], c),
                    "nw": pack_norm_w([np.asarray(ffn_norm_w, f32)[0], np.asarray(kv_norm_w, f32),
                                       np.asarray(attn_norm_w, f32)[1]])})
    resB = _run(_prog("B", lambda: build_ffn_phase(False, True, TQ, ag=USE_AG)), ims)
    sconst = sb_consts()
    w_kv = np.asarray(sb_w_kv, f32)
    w_q = np.asarray(sb_w_q, f32)[0]
    ims = []
    for c in range(8):
        b, hg = c // 4, c % 4
        hk = np.concatenate([np.asarray(resB[b * 4 + j]["post0"]) for j in range(4)], axis=1)
        hq = np.concatenate([np.asarray(resB[b * 4 + j]["post1"]) for j in range(4)], axis=1)
        im = {"hkT": np.ascontiguousarray(hk), "hqT": np.ascontiguousarray(hq),
              "wk": np.ascontiguousarray(w_kv[:, hg * 512:(hg + 1) * 512]),
              "wv": np.ascontiguousarray(w_kv[:, 2048 + hg * 512:2048 + (hg + 1) * 512]),
              "wq": np.ascontiguousarray(w_q[:, hg * 512:(hg + 1) * 512])}
        im.update(sconst)
        ims.append(im)
    resC = _run(_prog("C", lambda: build_sb_phase(S)), ims)
    ims = []
    for c in range(8):
        b, j = c // 4, c % 4
        sl = slice(j * TQ, (j + 1) * TQ)
        oT = np.concatenate([np.asarray(resC[b * 4 + hg]["oT"])[:, sl] for hg in range(4)], axis=0)
        ims.append({"oT": np.ascontiguousarray(oT), "hT": np.asarray(resB[c]["h2T"]),
                    "w_out": _rows(np.asarray(sb_w_out, f32)[0], c),
                    "w_gu": _rows(np.asarray(ffn_w_gate_up, f32)[1], c),
                    "w_dn": _rows(np.asarray(ffn_w_down, f32)[1], c),
                    "nw": pack_norm_w([np.asarray(ffn_norm_w, f32)[1], np.asarray(final_norm_w, f32)])})
    resD = _run(_prog("D", lambda: build_ffn_phase(True, False, TQ, ag=USE_AG)), ims)
    out = np.empty((B, S, D), f32)
    for c in range(8):
        b, j = c // 4, c % 4
        out[b, j * TQ:(j + 1) * TQ, :] = np.asarray(resD[c]["post0"]).T
    return out
```

```python
import numpy as np
from contextlib import ExitStack
import concourse.bass as bass
import concourse.mybir as mybir
from concourse.bass_utils import run_bass_kernel_spmd

F32 = mybir.dt.float32
BF16 = mybir.dt.bfloat16
AF = mybir.ActivationFunctionType
ALU = mybir.AluOpType

D = 2048
KC = 16
DFF = 5632
JC = 44
EPS = 1e-6
ENGS = ["pe", "act", "dve", "pool", "sp"]
SAME_ENG_SYNC = True


class Op:
    __slots__ = ("eng", "fn", "deps", "needs_inc", "dma_key", "val", "sem", "inc")

    def __init__(self, eng, fn, dma_key):
        self.eng = eng
        self.fn = fn
        self.deps = []
        self.needs_inc = False
        self.dma_key = dma_key
        self.val = 0
        self.sem = None


class Prog:
    def __init__(self, nc, stack):
        self.nc = nc
        self.stack = stack
        self.ops = {e: [] for e in ENGS}
        self.last_w = {}
        self.readers = {}
        self.dma_cnt = {}
        self.dma_sem = {}
        self.eng_sem = {}
        for e in ENGS:
            self.eng_sem[e] = stack.enter_context(nc.semaphore("prog_" + e))
        self.nps = 0

    def sb(self, name, shape, dtype):
        return self.stack.enter_context(self.nc.sbuf_tensor(name, list(shape), dtype))

    def ps(self, name, shape, dtype=F32):
        return self.stack.enter_context(self.nc.psum_tensor(name, list(shape), dtype))

    def add(self, eng, fn, reads=(), writes=(), dma_key=None, after=(), inc=16):
        op = Op(eng, fn, dma_key)
        op.inc = inc
        deps = {}

        def dep(d, kind):
            if d is None or d is op:
                return
            if d.dma_key is None and d.eng == eng:
                if eng == "pe" or kind == "war" or not SAME_ENG_SYNC:
                    return
            deps[id(d)] = d

        for r in reads:
            dep(self.last_w.get(r), "raw")
        for r in writes:
            dep(self.last_w.get(r), "waw")
            for rd in self.readers.get(r, ()):
                dep(rd, "war")
        for r in after:
            dep(self.last_w.get(r), "waw")
            for rd in self.readers.get(r, ()):
                dep(rd, "waw")
        for r in reads:
            self.readers.setdefault(r, []).append(op)
        for r in writes:
            self.last_w[r] = op
            self.readers[r] = []
        op.deps = list(deps.values())
        for d in op.deps:
            if d.dma_key is None:
                d.needs_inc = True
        if dma_key is not None:
            if dma_key not in self.dma_sem:
                self.dma_sem[dma_key] = self.stack.enter_context(
                    self.nc.semaphore("dma_%d" % len(self.dma_sem)))
                self.dma_cnt[dma_key] = 0
            self.dma_cnt[dma_key] += 1
            op.sem = self.dma_sem[dma_key]
            op.val = inc * self.dma_cnt[dma_key]
        else:
            op.sem = self.eng_sem[eng]
        self.ops[eng].append(op)
        return op

    def matmul(self, out, lhsT, rhs, start, stop, reads, writes):
        return self.add("pe", lambda e: e.matmul(out, lhsT, rhs, start=start, stop=stop),
                        reads, writes)

    def dma(self, eng, out, in_, reads, writes, key, after=()):
        return self.add(eng, lambda e: e.dma_start(out=out, in_=in_), reads, writes,
                        dma_key=key, after=after)

    def activation(self, eng, out, in_, func, reads, writes, bias=None, scale=None,
                   accum_out=None, after=()):
        kw = {}
        if bias is not None:
            kw["bias"] = bias
        if scale is not None:
            kw["scale"] = scale
        if accum_out is not None:
            kw["accum_out"] = accum_out
        return self.add(eng, lambda e: e.activation(out=out, in_=in_, func=func, **kw),
                        reads, writes, after=after)

    def tt(self, eng, out, in0, in1, op, reads, writes, after=()):
        return self.add(eng, lambda e: e.tensor_tensor(out=out, in0=in0, in1=in1, op=op),
                        reads, writes, after=after)

    def ts(self, eng, out, in0, s1, s2, op0, op1, reads, writes, after=()):
        if s2 is None:
            return self.add(eng, lambda e: e.tensor_scalar(out=out, in0=in0, scalar1=s1,
                                                           scalar2=None, op0=op0),
                            reads, writes, after=after)
        return self.add(eng, lambda e: e.tensor_scalar(out=out, in0=in0, scalar1=s1, scalar2=s2,
                                                       op0=op0, op1=op1), reads, writes,
                        after=after)

    def stt(self, eng, out, in0, scalar, in1, op0, op1, reads, writes, after=()):
        return self.add(eng, lambda e: e.scalar_tensor_tensor(out=out, in0=in0, scalar=scalar,
                                                              in1=in1, op0=op0, op1=op1),
                        reads, writes, after=after)

    def copy(self, eng, out, in_, reads, writes, after=()):
        if eng == "act":
            return self.add(eng, lambda e: e.activation(out=out, in_=in_, func=AF.Copy), reads,
                            writes, after=after)
        return self.add(eng, lambda e: e.tensor_copy(out=out, in_=in_), reads, writes,
                        after=after)

    def memset(self, eng, ap, val, writes):
        return self.add(eng, lambda e: e.memset(ap, val), (), writes)

    def allgather(self, out_t, in_t, reads, writes, groups=None):
        groups = groups or [list(range(8))]
        self.ncc = getattr(self, "ncc", 0) + 1
        return self.add("pool", lambda e: e.collective_compute(
            "AllGather", ALU.bypass, replica_groups=groups, ins=[in_t.ap().opt()],
            outs=[out_t.ap().opt()]), reads, writes, dma_key=("cc", self.ncc), inc=1)

    def wait_all(self, eng, keys):
        return self.add(eng, None, (), (), after=keys)

    def emit(self):
        nc = self.nc
        for e in ENGS:
            cnt = 0
            for op in self.ops[e]:
                if op.dma_key is None:
                    if op.needs_inc:
                        assert op.fn is not None
                        cnt += 1
                    op.val = cnt

        def body(e):
            def f(eng):
                waited = {}
                for op in self.ops[e]:
                    need = {}
                    for d in op.deps:
                        k = id(d.sem)
                        if k not in need or need[k][1] < d.val:
                            need[k] = (d.sem, d.val)
                    for k, (s, v) in need.items():
                        if waited.get(k, 0) < v:
                            eng.wait_ge(s, v)
                            waited[k] = v
                    if op.fn is not None:
                        inst = op.fn(eng)
                        if op.dma_key is not None and op.inc == 1:
                            inst.then_inc(op.sem)
                        elif op.dma_key is not None:
                            inst.then_inc(op.sem, 16)
                        elif op.needs_inc:
                            inst.then_inc(op.sem, 1)
            return f

        with nc.Block() as block:
            block.tensor(body("pe"))
            block.scalar(body("act"))
            block.vector(body("dve"))
            block.gpsimd(body("pool"))
            block.sync(body("sp"))


class Rot:
    def __init__(self, items):
        self.items = list(items)
        self.i = 0

    def next(self):
        x = self.items[self.i % len(self.items)]
        self.i += 1
        return x


def rmsnorm_stats_to_rstd(P, ss_banks, rstd, T, dim=D, keys=None):
    for hf in range(T // 512):
        sl = slice(hf * 512, (hf + 1) * 512)
        P.activation("act", rstd[:, sl], ss_banks[hf][:, :], AF.Sqrt, scale=1.0 / dim, bias=EPS,
                     reads=[keys[hf] if keys else ("bank", id(ss_banks[hf]))], writes=[("rstd", hf)])
        o_, i_ = rstd[:, sl], rstd[:, sl]
        P.add("dve", lambda e, o_=o_, i_=i_: e.reciprocal(out=o_, in_=i_),
              reads=[("rstd", hf)], writes=[("rstd", hf)])


def build_ffn_phase(final, extra_norms, T=1024, ag=False):
    nc = bass.Bass("TRN2", target_bir_lowering=False)
    NH = T // 512
    oT_d = nc.dram_tensor("oT", [D, T], BF16, kind="ExternalInput").ap()
    hT_d = nc.dram_tensor("hT", [D, T], F32, kind="ExternalInput").ap()
    wsh, wbn, wfl = {}, {}, {}
    for nm, (r, c) in (("w_out", (D, D)), ("w_gu", (D, 2 * DFF)), ("w_dn", (DFF, D))):
        if ag:
            wsh[nm] = nc.dram_tensor(nm, [r // 8, c], F32, kind="ExternalInput")
            wbn[nm] = nc.dram_tensor(nm + "_bn", [r // 8, c], F32)
            wfl[nm] = nc.dram_tensor(nm + "_full", [r, c], F32)
        else:
            wfl[nm] = nc.dram_tensor(nm, [r, c], F32, kind="ExternalInput")
    w_out_d, w_gu_d, w_dn_d = wfl["w_out"].ap(), wfl["w_gu"].ap(), wfl["w_dn"].ap()
    n_post = 2 if extra_norms else (1 if final else 0)
    nw_d = nc.dram_tensor("nw", [128, (1 + n_post) * KC], F32, kind="ExternalInput").ap()
    h1_d = nc.dram_tensor("h1_scr", [D, T], F32, kind="Internal").ap()
    h2_d = nc.dram_tensor("h2T", [D, T], F32, kind="ExternalOutput").ap()
    post_d = []
    for i in range(n_post):
        post_d.append(nc.dram_tensor("post%d" % i, [D, T], BF16 if extra_norms else F32,
                                     kind="ExternalOutput").ap())

    with ExitStack() as stack:
        P = Prog(nc, stack)
        big = P.sb("big", [128, JC * T], BF16)
        act = big[:, :].rearrange("p (j t) -> p j t", j=JC)
        hres = big[:, 0:KC * T * 2].bitcast(F32).rearrange("p (c t) -> p c t", c=KC)
        hn = P.sb("hn", [128, KC, T], BF16)
        NW = 3
        wslot = [P.sb("wslot%d" % i, [128, 11264], BF16) for i in range(NW)]
        wrot = Rot(range(NW))
        rstd = P.sb("rstd", [128, T], F32)
        sq = [P.sb("sq%d" % i, [128, 512], BF16) for i in range(2)]
        sqrot = Rot(range(2))
        sil = [P.sb("sil%d" % i, [128, 512], F32) for i in range(2)]
        silrot = Rot(range(2))
        h1t = [P.sb("h1t%d" % i, [128, 512], F32) for i in range(2)]
        h1rot = Rot(range(2))
        ot = [P.sb("ot%d" % i, [128, 512], F32) for i in range(2)]
        otrot = Rot(range(2))
        nw = P.sb("nwsb", [128, 1 + n_post, KC], F32)
        ones = P.sb("ones", [128, 128], BF16)
        banks = [P.ps("bank%d" % i, [128, 512], F32) for i in range(8)]

        def bk(i):
            return ("bank", id(banks[i]))

        hT_v = hT_d.rearrange("(c p) t -> p c t", p=128)
        oT_v = oT_d.rearrange("(c p) t -> p c t", p=128)
        h1_v = h1_d.rearrange("(c p) t -> p c t", p=128)
        h2_v = h2_d.rearrange("(c p) t -> p c t", p=128)

        for nm in ("w_out", "w_gu", "w_dn"):
            if ag:
                P.dma("sp", wbn[nm].ap(), wsh[nm].ap(), [], [("wbn", nm)], key=("wbn", nm))
        for nm in ("w_out", "w_gu", "w_dn"):
            if ag:
                P.allgather(wfl[nm], wbn[nm], [("wbn", nm)], [("wfull", nm)])
        P.memset("pool", ones[:, :], 1.0, writes=["ones"])
        P.dma("sp", nw[:, :, :], nw_d.rearrange("p (n c) -> p n c", c=KC), [], ["nw"], key="nw")
        for c in range(KC):
            P.dma("sp", hres[:, c, :], hT_v[:, c, :], [], [("h", c)], key=("h", c))
            P.dma("pool", hn[:, c, :], oT_v[:, c, :], [], [("hn", c)], key=("hn", c))

        w_out_v = w_out_d.rearrange("(kc p) n -> p kc n", p=128)
        ssb = [6, 7]
        brot = Rot([(0, 1), (2, 3), (4, 5)])
        for dg in range(8):
            s = wrot.next()
            wt = wslot[s][:, 0:KC * 256].rearrange("p (k n) -> p k n", k=KC)
            P.dma("pool", wt, w_out_v[:, :, dg * 256:(dg + 1) * 256], [("wfull", "w_out")], [("w", s)],
                  key=("w", s))
            for dd in range(2):
                d = dg * 2 + dd
                bset = brot.next()
                for k in range(KC):
                    for hf in range(NH):
                        P.matmul(banks[bset[hf]][:, :], wt[:, k, dd * 128:(dd + 1) * 128],
                                 hn[:, k, hf * 512:(hf + 1) * 512], k == 0, k == KC - 1,
                                 reads=[("w", s), ("hn", k)], writes=[bk(bset[hf])])
                for hf in range(NH):
                    sl = slice(hf * 512, (hf + 1) * 512)
                    P.tt("dve", hres[:, d, sl], banks[bset[hf]][:, :], hres[:, d, sl], ALU.add,
                         reads=[bk(bset[hf]), ("h", d)], writes=[("h", d)])
                    q = sqrot.next()
                    P.activation("act", sq[q][:, :], hres[:, d, sl], AF.Square,
                                 reads=[("h", d)], writes=[("sq", q)])
                    P.matmul(banks[ssb[hf]][:, :], ones[:, :], sq[q][:, :], d == 0, d == KC - 1,
                             reads=["ones", ("sq", q)], writes=[bk(ssb[hf])])
                P.dma("sp", h1_v[:, d, :], hres[:, d, :], [("h", d)], [("h1d", d)], key=("h1d", d))

        rmsnorm_stats_to_rstd(P, [banks[6], banks[7]], rstd, T)
        erot = Rot(["dve"])
        for c in range(KC):
            P.stt(erot.next(), hn[:, c, :], hres[:, c, :], nw[:, 0, c:c + 1], rstd[:, :],
                  ALU.mult, ALU.mult,
                  reads=[("h", c), "nw", ("rstd", 0), ("rstd", 1)], writes=[("hn", c)])

        w_gu_v = w_gu_d.rearrange("(kc p) n -> p kc n", p=128)
        b4rot = Rot([(0, 1, 2, 3), (4, 5, 6, 7)])
        for jp in range(JC // 2):
            s = wrot.next()
            wg = wslot[s][:, 0:KC * 256].rearrange("p (k n) -> p k n", k=KC)
            wu = wslot[s][:, KC * 256:2 * KC * 256].rearrange("p (k n) -> p k n", k=KC)
            P.dma("pool", wg, w_gu_v[:, :, jp * 256:(jp + 1) * 256], [("wfull", "w_gu")], [("w", s)],
                  key=("w", s))
            P.dma("pool", wu, w_gu_v[:, :, DFF + jp * 256:DFF + (jp + 1) * 256], [("wfull", "w_gu")],
                  [("w", s, 1)], key=("w", s, 1))
            for jj in range(2):
                j = jp * 2 + jj
                bset = b4rot.next()
                for gi, wv in enumerate((wg, wu)):
                    for k in range(KC):
                        for hf in range(NH):
                            b = bset[gi * 2 + hf]
                            P.matmul(banks[b][:, :], wv[:, k, jj * 128:(jj + 1) * 128],
                                     hn[:, k, hf * 512:(hf + 1) * 512], k == 0, k == KC - 1,
                                     reads=[("w", s), ("w", s, 1), ("hn", k)], writes=[bk(b)])
                for hf in range(NH):
                    sl = slice(hf * 512, (hf + 1) * 512)
                    si = silrot.next()
                    P.activation("act", sil[si][:, :], banks[bset[hf]][:, :], AF.Silu,
                                 reads=[bk(bset[hf])], writes=[("sil", si)])
                    P.tt("dve", act[:, j, sl], banks[bset[2 + hf]][:, :], sil[si][:, :], ALU.mult,
                         reads=[bk(bset[2 + hf]), ("sil", si)], writes=[("act", j)],
                         after=[("h", j // 2)])

        w_dn_v = w_dn_d.rearrange("(jc p) n -> p jc n", p=128)
        ssb = [6, 7]
        brot = Rot([(0, 1), (2, 3), (4, 5)])
        for dg in range(8):
            s = wrot.next()
            wt = wslot[s][:, 0:JC * 256].rearrange("p (k n) -> p k n", k=JC)
            P.dma("pool", wt[:, 0:22, :], w_dn_v[:, 0:22, dg * 256:(dg + 1) * 256], [("wfull", "w_dn")],
                  [("w", s)], key=("w", s))
            P.dma("pool", wt[:, 22:44, :], w_dn_v[:, 22:44, dg * 256:(dg + 1) * 256], [("wfull", "w_dn")],
                  [("w", s, 1)], key=("w", s, 1))
            for dd in range(2):
                d = dg * 2 + dd
                bset = brot.next()
                for j in range(JC):
                    for hf in range(NH):
                        P.matmul(banks[bset[hf]][:, :], wt[:, j, dd * 128:(dd + 1) * 128],
                                 act[:, j, hf * 512:(hf + 1) * 512], j == 0, j == JC - 1,
                                 reads=[("w", s), ("w", s, 1), ("act", j)], writes=[bk(bset[hf])])
                for hf in range(NH):
                    sl = slice(hf * 512, (hf + 1) * 512)
                    hi = h1rot.next()
                    P.dma("sp", h1t[hi][:, :], h1_v[:, d, sl], [("h1d", d)], [("h1t", hi)],
                          key=("h1t", hi))
                    oi = otrot.next()
                    P.tt("dve", ot[oi][:, :], banks[bset[hf]][:, :], h1t[hi][:, :], ALU.add,
                         reads=[bk(bset[hf]), ("h1t", hi)], writes=[("ot", oi)])
                    P.dma("sp", h2_v[:, d, sl], ot[oi][:, :], [("ot", oi)], [("h2d", d, hf)],
                          key=("ot", oi))
                    if n_post:
                        q = sqrot.next()
                        P.activation("act", sq[q][:, :], ot[oi][:, :], AF.Square,
                                     reads=[("ot", oi)], writes=[("sq", q)])
                        P.matmul(banks[ssb[hf]][:, :], ones[:, :], sq[q][:, :], d == 0, d == KC - 1,
                                 reads=["ones", ("sq", q)], writes=[bk(ssb[hf])])

        if n_post:
            rmsnorm_stats_to_rstd(P, [banks[6], banks[7]], rstd, T)
            pbuf = [big[:, i * 4 * T:(i * 4 + 2) * T].bitcast(F32) for i in range(4)]
            if extra_norms:
                obuf = [big[:, (i * 4 + 2) * T:(i * 4 + 3) * T] for i in range(4)]
            else:
                obuf = [big[:, (i * 4 + 2) * T:(i * 4 + 4) * T].bitcast(F32) for i in range(4)]
            prot = Rot(range(4))
            orot = Rot(range(4))
            allact = [("act", j) for j in range(JC)]
            for d in range(KC):
                pi = prot.next()
                P.dma("sp", pbuf[pi], h2_v[:, d, :], [("h2d", d, 0), ("h2d", d, 1)], [("pbuf", pi)],
                      key=("pbuf", pi), after=allact)
                for n in range(n_post):
                    oi = orot.next()
                    P.stt(erot.next(), obuf[oi], pbuf[pi], nw[:, 1 + n, d:d + 1], rstd[:, :],
                          ALU.mult, ALU.mult,
                          reads=[("pbuf", pi), "nw", ("rstd", 0), ("rstd", 1)], writes=[("obuf", oi)],
                          after=allact)
                    P.dma("sp", post_d[n].rearrange("(c p) t -> p c t", p=128)[:, d, :], obuf[oi],
                          [("obuf", oi)], [("postd", n, d)], key=("obuf", oi))
        outs = [("h2d", d, hf) for d in range(KC) for hf in range(NH)]
        outs += [("postd", n, d) for n in range(n_post) for d in range(KC)]
        P.wait_all("sp", outs)
        P.emit()
    return nc


def pack_norm_w(ws):
    arr = np.stack([np.asarray(w, np.float32).reshape(KC, 128).T for w in ws], axis=1)
    return np.ascontiguousarray(arr.reshape(128, -1))


GDK, GDV, GC = 256, 512, 64
WHC = 2 * GDK + 2 * GDV + 128


def gla_consts():
    j = np.arange(128)[:, None]
    i = np.arange(128)[None, :]
    same = (j // 64) == (i // 64)
    msuf = (same & (j > i)).astype(np.float32)
    cm = (same & (j <= i)).astype(np.float32)
    rm = np.ones((128, 512), np.float32)
    rm[:, ::64] = 0.0
    gi = np.zeros((128, 512), np.float32)
    gi[16, :] = 1.0
    hm = np.stack([(np.arange(128) < 64), (np.arange(128) >= 64)], 1).astype(np.float32)
    return {"msuf": msuf, "cmask": np.ascontiguousarray(cm), "rmask": rm, "glinit": gi,
            "hmask": np.ascontiguousarray(hm)}


def build_gla_phase(S=4096):
    nc = bass.Bass("TRN2", target_bir_lowering=False)
    NB = S // 512
    xT_d = nc.dram_tensor("xT", [D, S], F32, kind="ExternalInput").ap()
    wh_d = nc.dram_tensor("wh", [D, WHC], F32, kind="ExternalInput").ap()
    nw_d = nc.dram_tensor("nw", [128, KC], F32, kind="ExternalInput").ap()
    gw_d = nc.dram_tensor("gw", [128, 4], F32, kind="ExternalInput").ap()
    wgu_d = nc.dram_tensor("wgu", [17, GDK], F32, kind="ExternalInput").ap()
    msuf_d = nc.dram_tensor("msuf", [128, 128], F32, kind="ExternalInput").ap()
    cmask_d = nc.dram_tensor("cmask", [128, 128], F32, kind="ExternalInput").ap()
    glinit_d = nc.dram_tensor("glinit", [128, 512], F32, kind="ExternalInput").ap()
    hmask_d = nc.dram_tensor("hmask", [128, 2], F32, kind="ExternalInput").ap()
    rmask_d = nc.dram_tensor("rmask", [128, 512], F32, kind="ExternalInput").ap()
    og_d = nc.dram_tensor("ogT", [GDV, S], BF16, kind="ExternalOutput").ap()

    with ExitStack() as stack:
        P = Prog(nc, stack)
        W = P.sb("W", [128, KC, WHC], BF16)
        xt = [P.sb("xt%d" % i, [128, KC, 512], F32) for i in range(1)]
        sqall = P.sb("sqall", [128, KC * 512], BF16)
        aT = P.sb("aT", [128, KC, 512], BF16)
        rstd = P.sb("rstd", [128, 512], F32)
        nw = P.sb("nwsb", [128, KC], F32)
        gw = P.sb("gwsb", [128, 4], F32)
        wgu = P.sb("wgusb", [128, GDK], F32)
        hmask = P.sb("hmasksb", [128, 2], F32)
        msuf = P.sb("msufsb", [128, 128], F32)
        cmask = P.sb("cmasksb", [128, 128], F32)
        rmask = P.sb("rmasksb", [128, 512], F32)
        ones = P.sb("ones", [128, 128], BF16)
        glaug = P.sb("glaug", [128, 512], F32)
        ex = [P.sb("ex%d" % i, [128, 512], F32) for i in range(2)]
        sp = ex
        cs = [P.sb("cs%d" % i, [128, 512], F32) for i in range(2)]
        E1 = [P.sb("E1_%d" % i, [128, 512], F32) for i in range(2)]
        E2 = [P.sb("E2_%d" % i, [128, 512], F32) for i in range(2)]
        qe = [P.sb("qe%d" % i, [128, 512], BF16) for i in range(2)]
        ke = [P.sb("ke%d" % i, [128, 512], BF16) for i in range(2)]
        rsil = [P.sb("rsil%d" % i, [128, 512], F32) for i in range(4)]
        vtok = [P.sb("vtok%d" % i, [128, GDV], BF16) for i in range(4)]
        kd = [[P.sb("kd%d_%d" % (i, hh), [128, GDK], BF16) for hh in range(2)] for i in range(4)]
        spt = [P.sb("spt%d" % i, [128, GDK], F32) for i in range(4)]
        dk_ = [P.sb("dkt%d" % i, [128, GDK], F32) for i in range(2)]
        scsb = P.sb("scsb", [128, 128], BF16)
        S32 = [P.sb("S32_%d" % i, [128, GDV], F32) for i in range(2)]
        Sbf = [P.sb("Sbf_%d" % i, [128, GDV], BF16) for i in range(2)]
        ogf = [P.sb("ogf%d" % i, [128, 512], F32) for i in range(2)]
        ogb = [P.sb("ogb%d" % i, [128, 512], BF16) for i in range(2)]
        banks = [P.ps("bank%d" % i, [128, 512], F32) for i in range(8)]
        grot = Rot([0, 1, 2])
        SSB = 3

        def bk(i):
            return ("bank", i)

        P.memset("pool", ones[:, :], 1.0, writes=["ones"])
        P.dma("sp", glaug[:, :], glinit_d, [], ["glaug"], key="glaug")
        P.memset("pool", wgu[:, :], 0.0, writes=["wgu"])
        P.dma("sp", hmask[:, :], hmask_d, [], ["hmask"], key="hmask")
        for i in range(2):
            P.memset("pool", S32[i][:, :], 0.0, writes=[("S32", i)])
            P.memset("pool", Sbf[i][:, :], 0.0, writes=[("Sbf", i)])
        P.dma("sp", nw[:, :], nw_d, [], ["nw"], key="nw")
        P.dma("sp", gw[:, :], gw_d, [], ["gw"], key="gw")
        P.dma("sp", wgu[0:17, :], wgu_d, [], ["wgu"], key="wgu")
        P.dma("sp", msuf[:, :], msuf_d, [], ["msuf"], key="msuf")
        P.dma("sp", cmask[:, :], cmask_d, [], ["cmask"], key="cmask")
        P.dma("sp", rmask[:, :], rmask_d, [], ["rmask"], key="rmask")
        wh_v = wh_d.rearrange("(kc p) n -> p kc n", p=128)
        for k in range(KC):
            P.dma("pool", W[:, k, :], wh_v[:, k, :], [], [("W", k)], key=("W", k))
        xT_v = xT_d.rearrange("(c p) t -> p c t", p=128)
        og_v = og_d.rearrange("(c p) t -> p c t", p=128)
        Wk = [("W", k) for k in range(KC)]
        CQ, CK, CV, CR, CG = 0, GDK, 2 * GDK, 2 * GDK + GDV, 2 * GDK + 2 * GDV

        def load_x(n):
            xb = 0
            for c in range(KC):
                P.dma("sp", xt[xb][:, c, :], xT_v[:, c, n * 512:(n + 1) * 512], [],
                      [("xt", xb, c)], key=("xt", xb, c))

        def fm_proj(col0, M):
            b = grot.next()
            for k in range(KC):
                P.matmul(banks[b][0:M, :], W[:, k, col0:col0 + M], aT[:, k, :], k == 0, k == KC - 1,
                         reads=[("W", k), ("aT", k)], writes=[bk(b)])
            return b

        load_x(0)
        for n in range(NB):
            xb = 0
            xkeys = [("xt", xb, c) for c in range(KC)]
            P.activation("act", sqall[:, :], xt[xb][:, :, :].rearrange("p c t -> p (c t)"), AF.Square,
                         reads=xkeys, writes=["sqall"])
            for c in range(KC):
                P.matmul(banks[SSB][:, :], ones[:, :], sqall[:, c * 512:(c + 1) * 512], c == 0,
                         c == KC - 1, reads=["ones", "sqall"], writes=[bk(SSB)])
            rmsnorm_stats_to_rstd(P, [banks[SSB]], rstd, 512, keys=[bk(SSB)])
            for c in range(KC):
                P.stt("dve", aT[:, c, :], xt[xb][:, c, :], nw[:, c:c + 1], rstd[:, :], ALU.mult, ALU.mult,
                      reads=[("xt", xb, c), "nw", ("rstd", 0)], writes=[("aT", c)])
            if n + 1 < NB:
                load_x(n + 1)
            b = fm_proj(CG, 128)
            P.copy("dve", glaug[0:16, :], banks[b][0:16, :], reads=[bk(b)], writes=["glaug"])
            for dkc in range(2):
                b = grot.next()
                P.matmul(banks[b][:, :], wgu[:, dkc * 128:(dkc + 1) * 128], glaug[:, :], True, True,
                         reads=["wgu", "glaug"], writes=[bk(b)])
                P.activation("act", ex[dkc][:, :], banks[b][:, :], AF.Exp, scale=-1.0,
                             reads=[bk(b)], writes=[("ex", dkc)])
            for dkc in range(2):
                P.activation("act", sp[dkc][:, :], ex[dkc][:, :], AF.Ln, bias=1.0,
                             reads=[("ex", dkc)], writes=[("ex", dkc), ("sp", dkc)])
                o_, m_, s_ = cs[dkc][:, :], rmask[:, :], sp[dkc][:, :]
                P.add("dve", lambda e, o_=o_, m_=m_, s_=s_: e.tensor_tensor_scan(
                    out=o_, data0=m_, data1=s_, initial=0.0, op0=ALU.mult, op1=ALU.add),
                    reads=["rmask", ("sp", dkc)], writes=[("cs", dkc)])
            for dkc in range(2):
                P.activation("act", E1[dkc][:, :], cs[dkc][:, :], AF.Exp, scale=-1.0 / 16,
                             reads=[("cs", dkc)], writes=[("E1", dkc)])
                P.activation("act", E2[dkc][:, :], cs[dkc][:, :], AF.Exp, scale=1.0 / 16,
                             reads=[("cs", dkc)], writes=[("E2", dkc)])
            for dkc in range(2):
                b = fm_proj(CQ + dkc * 128, 128)
                P.stt("dve", qe[dkc][:, :], banks[b][:, :], float(GDK) ** -0.5, E1[dkc][:, :],
                      ALU.mult, ALU.mult, reads=[bk(b), ("E1", dkc)], writes=[("qe", dkc)])
            for dkc in range(2):
                b = fm_proj(CK + dkc * 128, 128)
                P.tt("dve", ke[dkc][:, :], banks[b][:, :], E2[dkc][:, :], ALU.mult,
                     reads=[bk(b), ("E2", dkc)], writes=[("ke", dkc)])
            for tt in range(4):
                tsl = slice(tt * 128, (tt + 1) * 128)
                b = grot.next()
                for k in range(KC):
                    P.matmul(banks[b][:, :], aT[:, k, tsl], W[:, k, CV:CV + GDV], k == 0, k == KC - 1,
                             reads=[("W", k), ("aT", k)], writes=[bk(b)])
                P.copy("act", vtok[tt][:, :], banks[b][:, :], reads=[bk(b)], writes=[("vtok", tt)])
                bg = grot.next()
                P.matmul(banks[bg][:, 0:GDK], glaug[:, tsl], wgu[:, :], True, True,
                         reads=["wgu", "glaug"], writes=[bk(bg)])
                P.activation("act", spt[tt][:, :], banks[bg][:, 0:GDK], AF.Exp, scale=-1.0,
                             reads=[bk(bg)], writes=[("spt", tt)])
                P.activation("act", spt[tt][:, :], spt[tt][:, :], AF.Ln, bias=1.0,
                             reads=[("spt", tt)], writes=[("spt", tt)])
            for tt in range(4):
                tsl = slice(tt * 128, (tt + 1) * 128)
                bs = grot.next()
                P.matmul(banks[bs][:, 0:GDK], msuf[:, :], spt[tt][:, :], True, True,
                         reads=["msuf", ("spt", tt)], writes=[bk(bs)])
                d_ = dk_[tt % 2]
                P.activation("act", d_[:, :], banks[bs][:, 0:GDK], AF.Exp, scale=-1.0 / 16,
                             reads=[bk(bs)], writes=[("dkt", tt % 2)])
                b = grot.next()
                for k in range(KC):
                    P.matmul(banks[b][:, 0:GDK], aT[:, k, tsl], W[:, k, CK:CK + GDK], k == 0, k == KC - 1,
                             reads=[("W", k), ("aT", k)], writes=[bk(b)])
                for hh in range(2):
                    P.stt("dve", kd[tt][hh][:, :], banks[b][:, 0:GDK], hmask[:, hh:hh + 1], d_[:, :],
                          ALU.mult, ALU.mult, reads=[bk(b), ("dkt", tt % 2), "hmask"],
                          writes=[("kd", tt, hh)])
            for dvc in range(4):
                b = fm_proj(CR + dvc * 128, 128)
                P.activation("act", rsil[dvc][:, :], banks[b][:, :], AF.Silu,
                             reads=[bk(b)], writes=[("rsil", dvc)])
            for tt in range(4):
                cols2 = slice(tt * 128, (tt + 1) * 128)
                for dkc in range(2):
                    P.matmul(banks[SSB][:, 0:128], ke[dkc][:, cols2], qe[dkc][:, cols2], dkc == 0, dkc == 1,
                             reads=[("ke", dkc), ("qe", dkc)], writes=[bk(SSB)])
                P.tt("dve", scsb[:, :], banks[SSB][:, 0:128], cmask[:, :], ALU.mult,
                     reads=[bk(SSB), "cmask"], writes=["scsb"])
                for dvc in range(4):
                    dsl = slice(dvc * 128, (dvc + 1) * 128)
                    P.matmul(banks[4 + dvc][:, cols2], vtok[tt][:, dsl], scsb[:, :], True, False,
                             reads=[("vtok", tt), "scsb"], writes=[("ob", dvc)])
                for hh in range(2):
                    c = tt * 2 + hh
                    cols = slice(c * 64, (c + 1) * 64)
                    for dvc in range(4):
                        dsl = slice(dvc * 128, (dvc + 1) * 128)
                        for dkc in range(2):
                            P.matmul(banks[4 + dvc][:, cols], Sbf[dkc][:, dsl], qe[dkc][:, cols], False,
                                     hh == 1 and dkc == 1,
                                     reads=[("Sbf", dkc), ("qe", dkc)], writes=[("ob", dvc)])
                    for dkc in range(2):
                        b = grot.next()
                        P.matmul(banks[b][:, :], kd[tt][hh][:, dkc * 128:(dkc + 1) * 128], vtok[tt][:, :],
                                 True, True, reads=[("kd", tt, hh), ("vtok", tt)], writes=[bk(b)])
                        lc = c * 64 + 63
                        P.stt("dve", S32[dkc][:, :], S32[dkc][:, :], E1[dkc][:, lc:lc + 1], banks[b][:, :],
                              ALU.mult, ALU.add, reads=[("S32", dkc), ("E1", dkc), bk(b)],
                              writes=[("S32", dkc)])
                        P.copy("pool", Sbf[dkc][:, :], S32[dkc][:, :], reads=[("S32", dkc)],
                               writes=[("Sbf", dkc)])
            for dvc in range(4):
                P.activation("act", sqall[:, dvc * 512:(dvc + 1) * 512], banks[4 + dvc][:, :], AF.Square,
                             reads=[("ob", dvc)], writes=["sqall"])
            for dvc in range(4):
                P.matmul(banks[SSB][:, :], ones[:, :], sqall[:, dvc * 512:(dvc + 1) * 512], dvc == 0,
                         dvc == 3, reads=["ones", "sqall"], writes=[bk(SSB)])
            rmsnorm_stats_to_rstd(P, [banks[SSB]], rstd, 512, dim=GDV, keys=[bk(SSB)])
            for dvc in range(4):
                P.stt("dve", ogf[dvc % 2][:, :], banks[4 + dvc][:, :], gw[:, dvc:dvc + 1], rstd[:, :],
                      ALU.mult, ALU.mult, reads=[("ob", dvc), "gw", ("rstd", 0)],
                      writes=[("ogf", dvc % 2)])
                P.tt("pool", ogb[dvc % 2][:, :], ogf[dvc % 2][:, :], rsil[dvc][:, :], ALU.mult,
                     reads=[("ogf", dvc % 2), ("rsil", dvc)], writes=[("ogb", dvc % 2)])
                P.dma("sp", og_v[:, dvc, n * 512:(n + 1) * 512], ogb[dvc % 2][:, :],
                      [("ogb", dvc % 2)], [("ogd", dvc, n)], key=("ogb", dvc % 2))
        P.wait_all("sp", [("ogd", dvc, n) for dvc in range(4) for n in range(NB)])
        P.emit()
    return nc


HD = 128
BIG = 30000.0


def sb_consts():
    p = np.arange(128)[:, None]
    f = np.arange(512)[None, :]
    m01 = np.stack([(p + 128 * off < f) for off in range(4)]).astype(np.float32)
    mbig = (1.0 - m01) * BIG
    tri = (np.arange(128)[:, None] >= np.arange(128)[None, :]).astype(np.float32)
    return {"m01": np.ascontiguousarray(m01.transpose(1, 0, 2).reshape(128, 4 * 512)),
            "mbig": np.ascontiguousarray(mbig.transpose(1, 0, 2).reshape(128, 4 * 512)),
            "tri": tri}


def build_sb_phase(S=4096):
    nc = bass.Bass("TRN2", target_bir_lowering=False)
    NB = S // 512
    NS = S // 128
    hk_d = nc.dram_tensor("hkT", [D, S], BF16, kind="ExternalInput").ap()
    hq_d = nc.dram_tensor("hqT", [D, S], BF16, kind="ExternalInput").ap()
    wk_d = nc.dram_tensor("wk", [D, 512], F32, kind="ExternalInput").ap()
    wv_d = nc.dram_tensor("wv", [D, 512], F32, kind="ExternalInput").ap()
    wq_d = nc.dram_tensor("wq", [D, 512], F32, kind="ExternalInput").ap()
    m01_d = nc.dram_tensor("m01", [128, 4 * 512], F32, kind="ExternalInput").ap()
    mbig_d = nc.dram_tensor("mbig", [128, 4 * 512], F32, kind="ExternalInput").ap()
    tri_d = nc.dram_tensor("tri", [128, 128], F32, kind="ExternalInput").ap()
    o_d = nc.dram_tensor("oT", [512, S], BF16, kind="ExternalOutput").ap()

    with ExitStack() as stack:
        P = Prog(nc, stack)
        nkT = P.sb("nkT", [128, 4, S], BF16)
        qT = P.sb("qT", [128, 4, S], BF16)
        vtok = P.sb("vtok", [128, NS, 512], BF16)
        ov = P.sb("ov", [128, 40960], BF16)
        wk = ov[:, 0:8192].rearrange("p (k n) -> p k n", k=KC)
        wv = ov[:, 8192:16384].rearrange("p (k n) -> p k n", k=KC)
        wq = ov[:, 16384:24576].rearrange("p (k n) -> p k n", k=KC)
        hs = [ov[:, 24576 + i * 8192:24576 + (i + 1) * 8192].rearrange("p (k n) -> p k n", k=KC)
              for i in range(2)]
        OVK = ["wk", "wv", "wq", ("hs", 0), ("hs", 1)]
        ef = [ov[:, i * 1024:(i + 1) * 1024].bitcast(F32) for i in range(2)]
        rbm = [ov[:, 2048 + i * 1024:2048 + (i + 1) * 1024].bitcast(F32) for i in range(2)]
        splf = [ov[:, 4096 + i * 1024:4096 + (i + 1) * 1024].bitcast(F32) for i in range(2)]
        splb = [ov[:, 6144 + i * 512:6144 + (i + 1) * 512] for i in range(3)]
        accb = [ov[:, 7680 + i * 512:7680 + (i + 1) * 512] for i in range(2)]
        ab = [ov[:, 8704 + i * 512:8704 + (i + 1) * 512] for i in range(3)]
        osb = [ov[:, 10240 + i * 512:10240 + (i + 1) * 512] for i in range(2)]
        m01 = P.sb("m01sb", [128, 4, 512], F32)
        mbig = P.sb("mbigsb", [128, 4, 512], F32)
        trif = P.sb("trif", [128, 128], F32)
        tri = P.sb("trib", [128, 128], BF16)
        ones = P.sb("ones", [128, 128], BF16)
        banks = [P.ps("bank%d" % i, [128, 512], F32) for i in range(8)]

        def bk(i):
            return ("bank", i)

        P.memset("pool", ones[:, :], 1.0, writes=["ones"])
        P.dma("sp", m01[:, :, :], m01_d.rearrange("p (o f) -> p o f", o=4), [], ["m01"], key="m01")
        P.dma("sp", mbig[:, :, :], mbig_d.rearrange("p (o f) -> p o f", o=4), [], ["mbig"], key="mbig")
        P.dma("sp", trif[:, :], tri_d, [], ["trif"], key="trif")
        P.copy("dve", tri[:, :], trif[:, :], reads=["trif"], writes=["tri"])
        for nm, wt, wd in (("wk", wk, wk_d), ("wv", wv, wv_d), ("wq", wq, wq_d)):
            P.dma("pool", wt, wd.rearrange("(kc p) n -> p kc n", p=128), [], [nm], key=nm)
        hk_v = hk_d.rearrange("(c p) t -> p c t", p=128)
        hq_v = hq_d.rearrange("(c p) t -> p c t", p=128)
        o_v = o_d.rearrange("(h p) t -> p h t", p=128)

        prot = Rot([0, 1, 2, 3])
        erot = Rot(["act", "dve"])
        for n in range(NB):
            nsl = slice(n * 512, (n + 1) * 512)
            P.dma("sp", hs[0], hk_v[:, :, nsl], [], [("hs", 0)], key=("hs", 0))
            P.dma("sp", hs[1], hq_v[:, :, nsl], [], [("hs", 1)], key=("hs", 1))
            for h in range(4):
                b = prot.next()
                for k in range(KC):
                    P.matmul(banks[b][:, :], wk[:, k, h * 128:(h + 1) * 128], hs[0][:, k, :], k == 0,
                             k == KC - 1, reads=["wk", ("hs", 0)], writes=[bk(b)])
                P.activation("act", nkT[:, h, nsl], banks[b][:, :], AF.Copy, scale=-1.0,
                             reads=[bk(b)], writes=[("nkT", h, n)])
            for st in range(4):
                b = prot.next()
                for k in range(KC):
                    P.matmul(banks[b][:, :], hs[0][:, k, st * 128:(st + 1) * 128], wv[:, k, :], k == 0,
                             k == KC - 1, reads=["wv", ("hs", 0)], writes=[bk(b)])
                P.copy("dve", vtok[:, n * 4 + st, :], banks[b][:, :], reads=[bk(b)],
                       writes=[("vtok", n * 4 + st)])
            for h in range(4):
                b = prot.next()
                for k in range(KC):
                    P.matmul(banks[b][:, :], wq[:, k, h * 128:(h + 1) * 128], hs[1][:, k, :], k == 0,
                             k == KC - 1, reads=["wq", ("hs", 1)], writes=[bk(b)])
                P.ts("dve", qT[:, h, nsl], banks[b][:, :], float(HD) ** -0.5, None, ALU.mult, None,
                     reads=[bk(b)], writes=[("qT", h, n)])

        zrot = Rot([0, 1])
        rrot = Rot([2, 3])
        orot = Rot([4, 5])
        efr, rbr, sfr, sbr, acr, abr, osr = (Rot(range(2)), Rot(range(2)), Rot(range(2)),
                                             Rot(range(3)), Rot(range(2)), Rot(range(3)),
                                             Rot(range(2)))
        for h in range(4):
            for T in range(NB):
                tsl = slice(T * 512, (T + 1) * 512)
                ob = orot.next()
                imax = 4 * T + 3
                acc_i = None
                pend = None

                def front(i):
                    ssl = slice(i * 128, (i + 1) * 128)
                    off = i - 4 * T
                    zb = zrot.next()
                    P.matmul(banks[zb][:, :], nkT[:, h, ssl], qT[:, h, tsl], True, True,
                             reads=[("nkT", h, i // 4), ("qT", h, T)], writes=[bk(zb)])
                    e_i = efr.next()
                    P.activation("act", ef[e_i], banks[zb][:, :], AF.Exp, scale=-1.0,
                                 reads=[bk(zb)], writes=[("ef", e_i)], after=OVK)
                    s_i = sbr.next()
                    if off >= 0:
                        f_i = sfr.next()
                        P.activation("act", splf[f_i], ef[e_i], AF.Ln, bias=1.0,
                                     reads=[("ef", e_i)], writes=[("splf", f_i)], after=OVK)
                        P.tt("pool", splb[s_i], splf[f_i], m01[:, off, :], ALU.mult,
                             reads=[("splf", f_i), "m01"], writes=[("splb", s_i)], after=OVK)
                    else:
                        P.activation("act", splb[s_i], ef[e_i], AF.Ln, bias=1.0,
                                     reads=[("ef", e_i)], writes=[("splb", s_i)], after=OVK)
                    return s_i

                def back(i, s_i, acc_i):
                    ssl = slice(i * 128, (i + 1) * 128)
                    off = i - 4 * T
                    kq_r = [("nkT", h, i // 4), ("qT", h, T)]
                    rb = rrot.next()
                    P.matmul(banks[rb][:, :], tri[:, :], splb[s_i], True, False,
                             reads=["tri", ("splb", s_i)], writes=[bk(rb)])
                    if acc_i is not None:
                        P.matmul(banks[rb][:, :], ones[:, :], accb[acc_i], False, False,
                                 reads=["ones", ("accb", acc_i)], writes=[bk(rb)])
                    P.matmul(banks[rb][:, :], nkT[:, h, ssl], qT[:, h, tsl], False, True,
                             reads=kq_r, writes=[bk(rb)])
                    a_i = abr.next()
                    if off >= 0:
                        r_i = rbr.next()
                        P.tt("dve", rbm[r_i], banks[rb][:, :], mbig[:, off, :], ALU.add,
                             reads=[bk(rb), "mbig"], writes=[("rbm", r_i)], after=OVK)
                        P.activation("act", ab[a_i], rbm[r_i], AF.Exp, scale=-1.0,
                                     reads=[("rbm", r_i)], writes=[("ab", a_i)], after=OVK)
                    else:
                        P.activation("act", ab[a_i], banks[rb][:, :], AF.Exp, scale=-1.0,
                                     reads=[bk(rb)], writes=[("ab", a_i)], after=OVK)
                    P.matmul(banks[ob][:, :], vtok[:, i, h * 128:(h + 1) * 128], ab[a_i], i == imax, i == 0,
                             reads=[("vtok", i), ("ab", a_i)], writes=[bk(ob)])
                    if i > 0:
                        if acc_i is None:
                            acc_i = acr.next()
                            P.copy("pool", accb[acc_i], splb[s_i], reads=[("splb", s_i)],
                                   writes=[("accb", acc_i)], after=OVK)
                        else:
                            new = acr.next()
                            P.tt("pool", accb[new], accb[acc_i], splb[s_i], ALU.add,
                                 reads=[("accb", acc_i), ("splb", s_i)], writes=[("accb", new)],
                                 after=OVK)
                            acc_i = new
                    return acc_i

                s_cur = front(imax)
                for i in range(imax, -1, -1):
                    s_nxt = front(i - 1) if i > 0 else None
                    acc_i = back(i, s_cur, acc_i)
                    s_cur = s_nxt
                o_i = osr.next()
                P.copy("dve", osb[o_i], banks[ob][:, :], reads=[bk(ob)], writes=[("osb", o_i)], after=OVK)
                P.dma("sp", o_v[:, h, tsl], osb[o_i], [("osb", o_i)], [("od", h, T)], key=("osb", o_i))
        P.wait_all("sp", [("od", h, T) for h in range(4) for T in range(NB)])
        P.emit()
    return nc


_PROGS = {}


def _prog(name, fn):
    if name not in _PROGS:
        _PROGS[name] = fn()
    return _PROGS[name]


USE_AG = False


def _rows(w, c):
    if not USE_AG:
        return w
    r = w.shape[0] // 8
    return np.ascontiguousarray(w[c * r:(c + 1) * r])


def _run(nc, in_maps):
    res = run_bass_kernel_spmd(nc, in_maps, core_ids=list(range(8)))
    return res.results


def kernel(x, attn_norm_w, ffn_norm_w, gla_w_in, gla_w_gate_up, gla_b_gate, gla_gnorm_w, gla_w_out,
           kv_norm_w, sb_w_kv, sb_w_q, sb_w_out, ffn_w_gate_up, ffn_w_down, final_norm_w):
    f32 = np.float32
    x = np.asarray(x, f32)
    B, S, _ = x.shape
    TQ = S // 4
    w_in = np.asarray(gla_w_in, f32)[0]
    gconst = gla_consts()
    xT = [np.ascontiguousarray(x[b].T) for b in range(B)]
    ims = []
    for c in range(8):
        b, h = c // 4, c % 4
        cols = np.concatenate([np.arange(h * 256, (h + 1) * 256), 1024 + np.arange(h * 256, (h + 1) * 256),
                               2048 + np.arange(h * 512, (h + 1) * 512),
                               4096 + np.arange(h * 512, (h + 1) * 512), 6144 + np.arange(16),
                               np.arange(h * 256, h * 256 + 112)])
        wgu = np.concatenate([np.asarray(gla_w_gate_up, f32)[0][:, h * 256:(h + 1) * 256],
                              np.asarray(gla_b_gate, f32)[0][None, h * 256:(h + 1) * 256]], 0)
        im = {"xT": xT[b], "wh": np.ascontiguousarray(w_in[:, cols]),
              "nw": pack_norm_w([np.asarray(attn_norm_w, f32)[0]]),
              "gw": np.ascontiguousarray(np.asarray(gla_gnorm_w, f32)[0].reshape(4, 128).T),
              "wgu": np.ascontiguousarray(wgu)}
        im.update(gconst)
        ims.append(im)
    resA = _run(_prog("A", lambda: build_gla_phase(S)), ims)
    ims = []
    for c in range(8):
        b, j = c // 4, c % 4
        sl = slice(j * TQ, (j + 1) * TQ)
        oT = np.concatenate([np.asarray(resA[b * 4 + h]["ogT"])[:, sl] for h in range(4)], axis=0)
        ims.append({"oT": np.ascontiguousarray(oT), "hT": np.ascontiguousarray(xT[b][:, sl]),
                    "w_out": _rows(np.asarray(gla_w_out, f32)[0], c),
                    "w_gu": _rows(np.asarray(ffn_w_gate_up, f32)[0], c),
                    "w_dn": _rows(np.asarray(ffn_w_down, f32)[0], c),
                    "nw": pack_norm_w([np.asarray(ffn_norm_w, f32)[0], np.asarray(kv_norm_w, f32),
                                       np.asarray(attn_norm_w, f32)[1]])})
    resB = _run(_prog("B", lambda: build_ffn_phase(False, True, TQ, ag=USE_AG)), ims)
    sconst = sb_consts()
    w_kv = np.asarray(sb_w_kv, f32)
    w_q = np.asarray(sb_w_q, f32)[0]
    ims = []
    for c in range(8):
        b, hg = c // 4, c % 4
        hk = np.concatenate([np.asarray(resB[b * 4 + j]["post0"]) for j in range(4)], axis=1)
        hq = np.concatenate([np.asarray(resB[b * 4 + j]["post1"]) for j in range(4)], axis=1)
        im = {"hkT": np.ascontiguousarray(hk), "hqT": np.ascontiguousarray(hq),
              "wk": np.ascontiguousarray(w_kv[:, hg * 512:(hg + 1) * 512]),
              "wv": np.ascontiguousarray(w_kv[:, 2048 + hg * 512:2048 + (hg + 1) * 512]),
              "wq": np.ascontiguousarray(w_q[:, hg * 512:(hg + 1) * 512])}
        im.update(sconst)
        ims.append(im)
    resC = _run(_prog("C", lambda: build_sb_phase(S)), ims)
    ims = []
    for c in range(8):
        b, j = c // 4, c % 4
        sl = slice(j * TQ, (j + 1) * TQ)
        oT = np.concatenate([np.asarray(resC[b * 4 + hg]["oT"])[:, sl] for hg in range(4)], axis=0)
        ims.append({"oT": np.ascontiguousarray(oT), "hT": np.asarray(resB[c]["h2T"]),
                    "w_out": _rows(np.asarray(sb_w_out, f32)[0], c),
                    "w_gu": _rows(np.asarray(ffn_w_gate_up, f32)[1], c),
                    "w_dn": _rows(np.asarray(ffn_w_down, f32)[1], c),
                    "nw": pack_norm_w([np.asarray(ffn_norm_w, f32)[1], np.asarray(final_norm_w, f32)])})
    resD = _run(_prog("D", lambda: build_ffn_phase(True, False, TQ, ag=USE_AG)), ims)
    out = np.empty((B, S, D), f32)
    for c in range(8):
        b, j = c // 4, c % 4
        out[b, j * TQ:(j + 1) * TQ, :] = np.asarray(resD[c]["post0"]).T
    return out
```
